# Optimizing a Trainium2 kernel written in Bass

```python
import math
import jax, jax.numpy as jnp
from jax import lax
import numpy as np

D_MODEL = 1024
BATCH = 8
SEQ = 4096
DEPTH = 2

CHUNK = 64
PLE_DIM = 256
D_POOL = D_MODEL // 2
D_CONV = D_MODEL - D_POOL
POOL_WINDOWS = (2, 4, 8, 16)
POOL_GROUP = D_POOL // len(POOL_WINDOWS)
CONV_KERNEL = 31
HEAD_DIM = 64
N_HEADS = D_MODEL // HEAD_DIM
LEFT_CHUNKS = 8
BAND = (LEFT_CHUNKS + 1) * CHUNK
MAX_REL_DIST = 256
D_FF = ((8 * D_MODEL // 3) + 127) // 128 * 128
FFN_CONV_KERNEL = 3
N_EVEN = (DEPTH + 1) // 2
N_ODD = DEPTH // 2
DEEPNORM_ALPHA = (2 * DEPTH) ** 0.25
DEEPNORM_BETA = (8 * DEPTH) ** -0.25
LN_EPS = 1e-5
NEG_INF = -1e30

kernel_name = "hybrid_pool_conv_chunkattn_encoder"


def layer_norm(x, g, b):
    x32 = x.astype(jnp.float32)
    mu = jnp.mean(x32, axis=-1, keepdims=True)
    var = jnp.mean(jnp.square(x32 - mu), axis=-1, keepdims=True)
    y = (x32 - mu) * lax.rsqrt(var + LN_EPS)
    return (y * g.astype(jnp.float32) + b.astype(jnp.float32)).astype(x.dtype)


def causal_dwconv(x, w, b):
    k = w.shape[0]
    c = x.shape[-1]
    y = lax.conv_general_dilated(
        x, w[:, None, :].astype(x.dtype), window_strides=(1,), padding=[(k - 1, 0)],
        dimension_numbers=('NWC', 'WIO', 'NWC'), feature_group_count=c)
    return y + b


def pool_conv_mixer(x, w_in, pool_w, pool_scale, dw_w, dw_b, cn_g, cn_b, w_out):
    bsz, s, _ = x.shape
    u = x @ w_in
    a = u[..., :D_POOL]
    b_val = u[..., D_POOL:D_POOL + D_CONV]
    b_gate = u[..., D_POOL + D_CONV:]

    a32 = a.astype(jnp.float32)
    cs = jnp.pad(jnp.cumsum(a32, axis=1), ((0, 0), (1, 0), (0, 0)))
    pos = jnp.arange(1, s + 1, dtype=jnp.float32)
    groups = []
    for g, w in enumerate(POOL_WINDOWS):
        sl = slice(g * POOL_GROUP, (g + 1) * POOL_GROUP)
        csg = cs[..., sl]
        lower = jnp.pad(csg[:, :s + 1 - w], ((0, 0), (w - 1, 0), (0, 0)))
        mean = (csg[:, 1:] - lower) / jnp.minimum(pos, float(w))[None, :, None]
        groups.append(mean - a32[..., sl])
    d = jnp.stack(groups, axis=2).astype(x.dtype)
    y_a = jnp.einsum('bsgc,gcd->bsgd', d, pool_w).reshape(bsz, s, D_POOL) * pool_scale

    glu = b_val * jax.nn.sigmoid(b_gate)
    h = causal_dwconv(glu, dw_w, dw_b)
    y_b = jax.nn.silu(layer_norm(h, cn_g, cn_b))

    return jnp.concatenate([y_a, y_b], axis=-1) @ w_out


def chunked_rel_attention(x, w_qkv, rel_bias, w_o):
    bsz, s, _ = x.shape
    nc = s // CHUNK
    pad = LEFT_CHUNKS * CHUNK
    q, k, v = jnp.split(x @ w_qkv, 3, axis=-1)
    q = q.reshape(bsz, nc, CHUNK, N_HEADS, HEAD_DIM).transpose(1, 0, 2, 3, 4)
    k = jnp.pad(k.reshape(bsz, s, N_HEADS, HEAD_DIM), ((0, 0), (pad, 0), (0, 0), (0, 0)))
    v = jnp.pad(v.reshape(bsz, s, N_HEADS, HEAD_DIM), ((0, 0), (pad, 0), (0, 0), (0, 0)))

    qi = jnp.arange(CHUNK)[:, None]
    kj = jnp.arange(BAND)[None, :]
    rel = jnp.clip(pad + qi - kj, -MAX_REL_DIST, MAX_REL_DIST) + MAX_REL_DIST
    bias = rel_bias[:, rel].astype(jnp.float32)
    scale = HEAD_DIM ** -0.5

    def one_chunk(args):
        qc, c = args
        kb = lax.dynamic_slice_in_dim(k, c * CHUNK, BAND, axis=1)
        vb = lax.dynamic_slice_in_dim(v, c * CHUNK, BAND, axis=1)
        sc = jnp.einsum('bqhd,bkhd->bhqk', qc, kb).astype(jnp.float32) * scale + bias
        key_pos = c * CHUNK - pad + jnp.arange(BAND)
        sc = jnp.where((key_pos >= 0)[None, None, None, :], sc, NEG_INF)
        pr = jax.nn.softmax(sc, axis=-1).astype(vb.dtype)
        return jnp.einsum('bhqk,bkhd->bqhd', pr, vb)

    out = lax.map(one_chunk, (q, jnp.arange(nc)))
    out = out.transpose(1, 0, 2, 3, 4).reshape(bsz, s, D_MODEL)
    return out @ w_o


def conv_ffn(x, w_up, dw_w, dw_b, w_down):
    gate, val = jnp.split(x @ w_up, 2, axis=-1)
    gate = causal_dwconv(gate, dw_w, dw_b)
    return (jax.nn.gelu(gate) * val) @ w_down


def setup_inputs(seed: int = 0) -> dict:
    key = jax.random.key(seed)
    ks = jax.random.split(key, 32)
    f32 = jnp.float32

    def nrm(k, shape, scale):
        return jax.random.normal(k, shape, f32) * scale

    d_in_even = D_POOL + 2 * D_CONV
    return {
        "x": nrm(ks[0], (BATCH, SEQ, D_MODEL), 1.0),
        "p": nrm(ks[1], (DEPTH, BATCH, SEQ, PLE_DIM), 1.0),
        "mix_w_in": nrm(ks[2], (N_EVEN, D_MODEL, d_in_even), D_MODEL ** -0.5),
        "pool_w": nrm(ks[3], (N_EVEN, len(POOL_WINDOWS), POOL_GROUP, POOL_GROUP), POOL_GROUP ** -0.5),
        "pool_scale": 1.0 + nrm(ks[4], (N_EVEN, D_POOL), 0.1),
        "conv_dw_w": nrm(ks[5], (N_EVEN, CONV_KERNEL, D_CONV), CONV_KERNEL ** -0.5),
        "conv_dw_b": nrm(ks[6], (N_EVEN, D_CONV), 0.02),
        "conv_ln_g": 1.0 + nrm(ks[7], (N_EVEN, D_CONV), 0.02),
        "conv_ln_b": nrm(ks[8], (N_EVEN, D_CONV), 0.02),
        "mix_w_out": nrm(ks[9], (N_EVEN, D_MODEL, D_MODEL), D_MODEL ** -0.5 * DEEPNORM_BETA),
        "attn_w_qkv": nrm(ks[10], (N_ODD, D_MODEL, 3 * D_MODEL), D_MODEL ** -0.5),
        "attn_rel_bias": nrm(ks[11], (N_ODD, N_HEADS, 2 * MAX_REL_DIST + 1), 0.5),
        "attn_w_o": nrm(ks[12], (N_ODD, D_MODEL, D_MODEL), D_MODEL ** -0.5 * DEEPNORM_BETA),
        "ln_mix_g": 1.0 + nrm(ks[13], (DEPTH, D_MODEL), 0.02),
        "ln_mix_b": nrm(ks[14], (DEPTH, D_MODEL), 0.02),
        "ffn_w_up": nrm(ks[15], (DEPTH, D_MODEL, 2 * D_FF), D_MODEL ** -0.5),
        "ffn_dw_w": nrm(ks[16], (DEPTH, FFN_CONV_KERNEL, D_FF), FFN_CONV_KERNEL ** -0.5),
        "ffn_dw_b": nrm(ks[17], (DEPTH, D_FF), 0.02),
        "ffn_w_down": nrm(ks[18], (DEPTH, D_FF, D_MODEL), D_FF ** -0.5 * DEEPNORM_BETA),
        "ple_w_proj": nrm(ks[19], (DEPTH, PLE_DIM, D_MODEL), PLE_DIM ** -0.5),
        "ple_w_gate": nrm(ks[20], (DEPTH, D_MODEL, D_MODEL), D_MODEL ** -0.5),
        "ple_b_gate": nrm(ks[21], (DEPTH, D_MODEL), 0.02),
        "ln_ffn_g": 1.0 + nrm(ks[22], (DEPTH, D_MODEL), 0.02),
        "ln_ffn_b": nrm(ks[23], (DEPTH, D_MODEL), 0.02),
    }


def reference(x, p, mix_w_in, pool_w, pool_scale, conv_dw_w, conv_dw_b, conv_ln_g,
              conv_ln_b, mix_w_out, attn_w_qkv, attn_rel_bias, attn_w_o, ln_mix_g,
              ln_mix_b, ffn_w_up, ffn_dw_w, ffn_dw_b, ffn_w_down, ple_w_proj,
              ple_w_gate, ple_b_gate, ln_ffn_g, ln_ffn_b):
    for i in range(DEPTH):
        j = i // 2
        if i % 2 == 0:
            mix = pool_conv_mixer(x, mix_w_in[j], pool_w[j], pool_scale[j], conv_dw_w[j],
                                  conv_dw_b[j], conv_ln_g[j], conv_ln_b[j], mix_w_out[j])
        else:
            mix = chunked_rel_attention(x, attn_w_qkv[j], attn_rel_bias[j], attn_w_o[j])
        x = layer_norm(DEEPNORM_ALPHA * x + mix, ln_mix_g[i], ln_mix_b[i])
        ffn = conv_ffn(x, ffn_w_up[i], ffn_dw_w[i], ffn_dw_b[i], ffn_w_down[i])
        gate = jax.nn.sigmoid(x @ ple_w_gate[i] + ple_b_gate[i])
        ple = gate * (p[i] @ ple_w_proj[i])
        x = layer_norm(DEEPNORM_ALPHA * x + ffn + ple, ln_ffn_g[i], ln_ffn_b[i])
    return x
```

```python
from contextlib import ExitStack
import numpy as np
import concourse.bass as bass
import concourse.mybir as mybir
from concourse.bass_utils import run_bass_kernel_spmd

F32 = mybir.dt.float32
BF16 = mybir.dt.bfloat16
AF = mybir.ActivationFunctionType
ALU = mybir.AluOpType

D = 1024
SEQ = 4096
T = 512
DFF = 2816
NFF = 22
ALPHA = float(4.0 ** 0.25)
EPS = 1e-5
NBLK = 59
B_WIN, B_DIAG, B_WOUT, B_FFN0, B_K, B_V, B_Q, B_WO, B_FFN1 = 0, 3, 7, 9, 30, 32, 34, 36, 38
RING = 3
MASKV = -30000.0
ARENA_BYTES = 207 * 1024


class View:
    __slots__ = ("ap", "res")

    def __init__(self, ap, res):
        self.ap = ap
        self.res = res


class Buf:
    def __init__(self, ctx, name, off, ncols, dt):
        self.ctx, self.name, self.off, self.n, self.dt = ctx, name, off, ncols, dt
        self.esz = 4 if dt is F32 else 2

    def v(self, a=0, b=None, p0=0, p1=128):
        if b is None:
            b = self.n
        assert 0 <= a < b <= self.n, (self.name, a, b, self.n)
        base = self.ctx["f32"] if self.dt is F32 else self.ctx["bf"]
        e0 = self.off // self.esz
        lo = self.off + a * self.esz
        hi = self.off + b * self.esz
        return View(base[p0:p1, e0 + a:e0 + b], tuple(range(lo // 256, (hi - 1) // 256 + 1)))


class Sched:
    def __init__(self):
        self.ops = []
        self.lastw = {}
        self.lastr = {}
        self.total_keys = set()
        self.phase = ""
        self.phases = []
        self.names = {}
        self.sig_ok = []

    def op(self, eng, fn, r=(), w=(), dma=None, sig_ok=True):
        i = len(self.ops)
        raw, war = set(), set()
        lastw, lastr = self.lastw, self.lastr
        for v in r:
            for c in v.res:
                x = lastw.get(c)
                if x is not None:
                    raw.add(x)
        for v in w:
            for c in v.res:
                x = lastw.get(c)
                if x is not None:
                    raw.add(x)
                d = lastr.get(c)
                if d:
                    war.update(d.values())
        rk = ("dma", i) if dma else eng
        for v in r:
            for c in v.res:
                d = lastr.get(c)
                if d is None:
                    d = lastr[c] = {}
                d[rk] = i
        for v in w:
            for c in v.res:
                lastw[c] = i
                lastr[c] = {}
        raw.discard(i)
        war.discard(i)
        war -= raw
        self.ops.append((eng, fn, raw, war, dma))
        self.sig_ok.append(sig_ok)
        self.phases.append(self.phase)
        return i

    def _needs_wait(self, i, d, kind):
        ei, di = self.ops[i], self.ops[d]
        if di[4] is not None:
            return True
        if ei[0] == di[0]:
            if ei[4] is not None:
                return True
            if ei[0] == "pe":
                return False
            if ei[0] in ("pool", "dve"):
                return True
            return kind == "raw"
        return True

    def emit(self, nc):
        ops = self.ops
        n = len(ops)
        need_sig = [False] * n
        deps = [None] * n
        nxt_ok = [None] * n
        last = {}
        for i in range(n - 1, -1, -1):
            e = ops[i][0]
            if self.sig_ok[i] and ops[i][4] is None and ops[i][1] is not None:
                last[e] = i
            nxt_ok[i] = last.get(e)

        def redirect(i, d):
            if ops[d][4] is not None or self.sig_ok[d]:
                return d
            d2 = nxt_ok[d]
            if d2 is not None and d2 < i:
                return d2
            return d
        for i, o in enumerate(ops):
            dl = []
            for d in o[2]:
                if self._needs_wait(i, d, "raw"):
                    dl.append(redirect(i, d))
            for d in o[3]:
                if self._needs_wait(i, d, "war"):
                    dl.append(redirect(i, d))
            deps[i] = dl
            for d in dl:
                need_sig[d] = True
        cnt = {}
        sig = [None] * n
        for i, o in enumerate(ops):
            if o[4] is not None:
                k = ("dma", o[4])
                cnt[k] = cnt.get(k, 0) + 16
                sig[i] = (k, cnt[k])
            elif need_sig[i]:
                assert o[1] is not None
                k = ("eng", o[0])
                cnt[k] = cnt.get(k, 0) + 1
                sig[i] = (k, cnt[k])
        for i, o in enumerate(ops):
            if o[4] is not None and o[4] in self.total_keys:
                k = ("dma", o[4])
                sig[i] = (k, cnt[k])
        with ExitStack() as st:
            sems = {}
            for k in cnt:
                sems[k] = st.enter_context(nc.semaphore("s_%s_%s" % k))
            block = st.enter_context(nc.Block())
            decos = [("pe", block.tensor), ("act", block.scalar), ("dve", block.vector),
                     ("pool", block.gpsimd), ("sp", block.sync)]
            for engname, deco in decos:
                def body(e, engname=engname):
                    seen = {}
                    for i, o in enumerate(ops):
                        if o[0] != engname:
                            continue
                        waits = {}
                        for d in deps[i]:
                            k, c = sig[d]
                            if waits.get(k, 0) < c:
                                waits[k] = c
                        for k, c in waits.items():
                            if seen.get(k, 0) < c:
                                e.wait_ge(sems[k], c)
                                seen[k] = c
                        if o[1] is not None:
                            try:
                                self.names[i] = nc.get_next_instruction_name()
                            except Exception:
                                pass
                            ins = o[1](e)
                            if sig[i] is not None:
                                ins.then_inc(sems[sig[i][0]], 16 if o[4] is not None else 1)
                deco(body)
        return cnt


def build(NT):
    nc = bass.Bass("TRN2", target_bir_lowering=False)
    SC = NT * T
    NV = VEC_COLS["_total"]
    xT = nc.dram_tensor("xT", [D, SC], F32, kind="ExternalInput").ap()
    pT = nc.dram_tensor("pT", [2 * 256, SC], F32, kind="ExternalInput").ap()
    wsrc = nc.dram_tensor("wsrc", [NBLK * 128, 4096], F32, kind="ExternalInput").ap()
    wsm = nc.dram_tensor("wsm", [128, 512 + 4096], F32, kind="ExternalInput").ap()
    vecs = nc.dram_tensor("vecs", [128, NV], F32, kind="ExternalInput").ap()
    cst = nc.dram_tensor("cst", [128, 208], F32, kind="ExternalInput").ap()
    btab = nc.dram_tensor("btab", [128, 16 * 640], F32, kind="ExternalInput").ap()
    outT = nc.dram_tensor("outT", [D, SC], F32, kind="ExternalOutput").ap()
    wc = nc.dram_tensor("wc", [NBLK * 128, 4096], BF16).ap()

    arena = nc.alloc_sbuf_tensor("arena", [128, ARENA_BYTES // 4], F32)
    ctx = {"f32": arena, "bf": arena.bitcast(BF16)}
    ps_t = [nc.alloc_psum_tensor("psb%d" % i, [128, 512], F32) for i in range(8)]

    class PB:
        def __init__(self, i):
            self.i = i

        def v(self, a=0, b=512, p0=0, p1=128):
            return View(ps_t[self.i][p0:p1, a:b], ("ps%d" % self.i,))

    PBK = [PB(i) for i in range(8)]
    S = Sched()
    off = [0]

    def alloc(name, ncols, dt):
        esz = 4 if dt is F32 else 2
        b = Buf(ctx, name, off[0], ncols, dt)
        off[0] += (ncols * esz + 255) // 256 * 256
        assert off[0] <= ARENA_BYTES, (name, off[0])
        return b

    cstb = alloc("cst", 208, F32)
    identb = alloc("identb", 128, BF16)
    vec = alloc("vec", NV, F32)
    onesA = alloc("onesA", 128, BF16)
    onesB = alloc("onesB", 128, BF16)
    ones64 = alloc("ones64", 64, BF16)
    epsb = alloc("eps", 2, F32)
    wsmb = alloc("wsmb", 512 + 4096, BF16)
    Bt = alloc("Bt", 16 * 640, BF16)
    xres = alloc("xres", 8 * T, F32)
    xb = alloc("xb", 8 * T, BF16)
    a_ext = alloc("a_ext", 4 * 527, F32)
    glu_ext = alloc("glu_ext", 4 * 542, BF16)
    gh = alloc("gh", 2 * NFF * 2, F32)
    kT = alloc("kT", 8 * 2 * T, BF16)
    Vt = alloc("Vt", 8 * 1024, BF16)
    pbuf = [alloc("p%d" % l, 2 * T, BF16) for l in range(2)]
    ring = [alloc("ring%d" % i, 4096, BF16) for i in range(RING)]
    mean_sb = alloc("mean_sb", T, F32)
    var_sb = alloc("var_sb", T, F32)
    rstd_sb = alloc("rstd_sb", T, F32)
    zb = [alloc("zb%d" % i, T, BF16) for i in range(2)]
    z2b = [alloc("z2b%d" % i, T, BF16) for i in range(2)]
    tmp16 = alloc("tmp16", 16, F32)
    base = off[0]
    ptA = alloc("ptA", 527, F32)
    ptB = alloc("ptB", 527, F32)
    dbuf = alloc("dbuf", 4 * T, BF16)
    sig = [alloc("sig%d" % i, T, F32) for i in range(2)]
    hbuf = alloc("h", 4 * T, F32)
    a_raw = alloc("a_raw", 4 * T, F32)
    ycat = alloc("ycat", 8 * T, BF16)
    end0 = off[0]
    off[0] = base
    gbuf = alloc("g", NFF * T, BF16)
    gext = [alloc("gext%d" % i, 514, F32) for i in range(2)]
    ctmp = [alloc("ctmp%d" % i, T, F32) for i in range(2)]
    gebuf = [alloc("ge%d" % i, T, F32) for i in range(2)]
    sgate = [alloc("sgate%d" % i, T, F32) for i in range(2)]
    plebuf = [alloc("ple%d" % i, T, F32) for i in range(4)]
    end1 = off[0]
    off[0] = base
    qT = alloc("qT", 8 * T, BF16)
    ao = alloc("ao", 8 * T, BF16)
    Pt = [alloc("Pt%d" % i, T, BF16) for i in range(6)]
    rec = [alloc("rec%d" % i, T, F32) for i in range(2)]
    end2 = off[0]
    off[0] = max(end0, end1, end2)
    assert off[0] <= ARENA_BYTES, off[0]

    PCE = ["pool"]
    rot = {}

    def nxt(name, lst):
        i = rot.get(name, 0)
        rot[name] = i + 1
        return lst[i % len(lst)]

    bank_ctr = [0]
    bank_pool = [list(range(6))]

    def nb():
        pool = bank_pool[0]
        b = pool[bank_ctr[0] % len(pool)]
        bank_ctr[0] += 1
        return PBK[b]

    def dview(ap, name):
        return View(ap, (name,))

    def MM(out, lhsT, rhs, start, stop, **kw):
        S.op("pe", lambda e: e.matmul(out.ap, lhsT=lhsT.ap, rhs=rhs.ap, start=start, stop=stop, **kw),
             r=[lhsT, rhs], w=[out], sig_ok=bool(stop))

    zero_col = epsb.v(1, 2)

    def ACT(out, in_, func, bias=None, scale=1.0):
        rr = [in_]
        if bias is None:
            bias = zero_col
        rr.append(bias)
        if isinstance(scale, View):
            rr.append(scale)
            sc = scale.ap
        else:
            sc = scale
        S.op("act", lambda e: e.activation(out=out.ap, in_=in_.ap, func=func, bias=bias.ap, scale=sc),
             r=rr, w=[out])

    def TT(eng, out, a, b, op):
        S.op(eng, lambda e: e.tensor_tensor(out=out.ap, in0=a.ap, in1=b.ap, op=op), r=[a, b], w=[out])

    def STT(eng, out, in0, scalar, in1, op0, op1):
        rr = [in0, in1]
        if isinstance(scalar, View):
            rr.append(scalar)
            sc = scalar.ap
        else:
            sc = scalar
        S.op(eng, lambda e: e.scalar_tensor_tensor(out=out.ap, in0=in0.ap, scalar=sc, in1=in1.ap, op0=op0, op1=op1),
             r=rr, w=[out])

    def TS(eng, out, in0, s1, s2, op0, op1=None):
        rr = [in0]
        a1 = s1
        a2 = s2
        if isinstance(s1, View):
            rr.append(s1)
            a1 = s1.ap
        if isinstance(s2, View):
            rr.append(s2)
            a2 = s2.ap
        if op1 is None:
            S.op(eng, lambda e: e.tensor_scalar(out=out.ap, in0=in0.ap, scalar1=a1, scalar2=None, op0=op0),
                 r=rr, w=[out])
        else:
            S.op(eng, lambda e: e.tensor_scalar(out=out.ap, in0=in0.ap, scalar1=a1, scalar2=a2, op0=op0, op1=op1),
                 r=rr, w=[out])

    def CP(eng, out, in_):
        S.op(eng, lambda e: e.tensor_copy(out=out.ap, in_=in_.ap), r=[in_], w=[out])

    def MSET(eng, out, val):
        S.op(eng, lambda e: e.memset(out.ap, val), r=[], w=[out])

    def RCP(out, in_):
        S.op("dve", lambda e: e.reciprocal(out=out.ap, in_=in_.ap), r=[in_], w=[out])

    def DMA(eng, out, in_, key):
        S.op(eng, lambda e: e.dma_start(out=out.ap, in_=in_.ap), r=[in_], w=[out], dma=key)

    def vcol(name, c=0):
        b0 = VEC_COLS[name]
        return vec.v(b0 + c, b0 + c + 1)

    def load_x(tt, xq="pool"):
        for kc in range(8):
            DMA(xq, xres.v(kc * T, (kc + 1) * T),
                dview(xT[kc * 128:(kc + 1) * 128, tt * T:(tt + 1) * T], "d_x"), ("xs%d" if xq == "sp" else "x%d") % kc)

    def load_xb(tt):
        for kc in range(8):
            DMA("pool", xb.v(kc * T, (kc + 1) * T),
                dview(xT[kc * 128:(kc + 1) * 128, tt * T:(tt + 1) * T], "d_x"), "xb%d" % kc)

    def load_p(tt, l):
        for kc in range(2):
            DMA("pool", pbuf[l].v(kc * T, (kc + 1) * T),
                dview(pT[l * 256 + kc * 128:l * 256 + (kc + 1) * 128, tt * T:(tt + 1) * T], "d_p"), "p%d_%d" % (l, kc))

    DMA("pool", cstb.v(), dview(cst, "d_cst"), "c0")
    DMA("pool", identb.v(), dview(cst[:, 0:128], "d_cst"), "c1")
    DMA("pool", vec.v(), dview(vecs, "d_vecs"), "c2")
    DMA("pool", wsmb.v(), dview(wsm, "d_wsm"), "c4")
    load_xb(0)
    load_x(0)
    load_p(0, 0)
    load_p(0, 1)
    ncols_blk = [BLK_NCOLS[b] for b in range(NBLK)]
    CVLA = NBLK
    cvstate = {"n": 0}

    def emit_conv_upto(bmax):
        while cvstate["n"] <= min(bmax, NBLK - 1):
            b = cvstate["n"]
            cvstate["n"] += 1
            if B_DIAG <= b < B_DIAG + 4:
                continue
            ncb = ncols_blk[b]
            DMA("pool", dview(wc[b * 128:(b + 1) * 128, 0:ncb], "wc%d" % b),
                dview(wsrc[b * 128:(b + 1) * 128, 0:ncb], "d_wsrc"), "cv%d" % b)
    MSET("dve", onesA.v(), 1.0 / 512)
    MSET("dve", onesB.v(), 1.0 / 1024)
    MSET("dve", ones64.v(), 1.0)
    MSET("dve", epsb.v(0, 1), EPS)
    MSET("dve", epsb.v(1, 2), 0.0)
    MSET("dve", a_ext.v(), 0.0)
    MSET("dve", glu_ext.v(), 0.0)
    MSET("dve", gh.v(), 0.0)
    for c5 in range(5):
        DMA("pool", hbuf.v(), dview(btab[:, c5 * 2048:(c5 + 1) * 2048], "d_btab"), "c3")
        ACT(Bt.v(c5 * 2048, (c5 + 1) * 2048), hbuf.v(), AF.Exp)
    identf = cstb.v(0, 128)
    invc = lambda a, b: cstb.v(128 + a, 128 + b)
    cwb = VEC_COLS["conv_dw_w"]
    PCE[0] = "dve"
    for j in range(4):
        gb = (j % 2) * 4096
        for k in range(31):
            i = k * 4 + j
            TS(PCE[0], gbuf.v(gb + k * 128, gb + (k + 1) * 128), identf, vec.v(cwb + i, cwb + i + 1), None, ALU.mult)
        b = B_DIAG + j
        DMA("sp", dview(wc[b * 128:(b + 1) * 128, 0:3968], "wc%d" % b), gbuf.v(gb, gb + 3968), "dg%d" % (j % 2))

    total_loads = NT * NBLK
    wstate = {"emitted": 0, "cur": 0}

    def emit_load(L):
        b = L % NBLK
        slot = ring[L % RING]
        ncb = ncols_blk[b]
        if L < NBLK and not (B_DIAG <= b < B_DIAG + 4):
            S.op("pool", lambda e: e.dma_start(out=slot.v(0, ncb).ap, in_=wsrc[b * 128:(b + 1) * 128, 0:ncb]),
                 r=[View(None, ("d_wsrc",))], w=[slot.v(0, ncb)], dma="ringp%d" % (L % RING))
            S.op("sp", lambda e: e.dma_start(out=wc[b * 128:(b + 1) * 128, 0:ncb], in_=slot.v(0, ncb).ap),
                 r=[slot.v(0, ncb)], w=[View(None, ("wc%d" % b,))], dma="wb%d" % (L % RING))
            return
        S.op("sp", lambda e: e.dma_start(out=slot.v(0, ncb).ap, in_=wc[b * 128:(b + 1) * 128, 0:ncb]),
             r=[View(None, ("wc%d" % b,))], w=[slot.v(0, ncb)], dma="ring%d" % (L % RING))

    def wacq(expect_b):
        L = wstate["cur"]
        assert L % NBLK == expect_b, (L, expect_b)
        while wstate["emitted"] <= L:
            emit_load(wstate["emitted"])
            wstate["emitted"] += 1
        return ring[L % RING]

    def wrel():
        L = wstate["cur"]
        wstate["cur"] = L + 1
        nl = L + RING
        if nl < total_loads and wstate["emitted"] <= nl:
            while wstate["emitted"] <= nl:
                emit_load(wstate["emitted"])
                wstate["emitted"] += 1

    for L in range(min(RING, total_loads)):
        emit_load(L)
        wstate["emitted"] += 1

    def ln_feed_prep(zv):
        a = nxt("zb", zb)
        b = nxt("z2b", z2b)
        ACT(a.v(), zv, AF.Identity)
        ACT(b.v(), zv, AF.Square)
        return a.v(), b.v()

    def ln_stats_finish(bm, bq):
        ACT(var_sb.v(), bm.v(), AF.Square)
        ACT(mean_sb.v(), bm.v(), AF.Identity)
        STT("dve", var_sb.v(), bq.v(), EPS, var_sb.v(), ALU.add, ALU.subtract)
        ACT(var_sb.v(), var_sb.v(), AF.Ln)
        ACT(rstd_sb.v(), var_sb.v(), AF.Exp, scale=-0.5)

    POOL_CHUNKS = ()

    def ln_apply(gname, bname, final, tt_out=None):
        dch = [m for m in range(8) if m not in POOL_CHUNKS]
        subbed = set()

        def sub(m):
            eng = PCE[0] if m in POOL_CHUNKS else "dve"
            zv = xres.v(m * T, (m + 1) * T)
            TT(eng, zv, zv, mean_sb.v(), ALU.subtract)
            subbed.add(m)
        for m in dch[:2]:
            sub(m)
        for m in range(8):
            eng = PCE[0] if m in POOL_CHUNKS else "dve"
            zv = xres.v(m * T, (m + 1) * T)
            if m not in subbed:
                sub(m)
            TT(eng, zv, zv, rstd_sb.v(), ALU.mult)
            if m in dch:
                i = dch.index(m)
                if i + 2 < len(dch) and dch[i + 2] not in subbed:
                    sub(dch[i + 2])
            if final:
                ACT(zv, zv, AF.Identity, bias=vcol(bname, m), scale=vcol(gname, m))
                DMA("act", dview(outT[m * 128:(m + 1) * 128, tt_out * T:(tt_out + 1) * T], "d_out"), zv, "o%d" % m)
            else:
                ACT(xb.v(m * T, (m + 1) * T), zv, AF.Identity, bias=vcol(bname, m), scale=vcol(gname, m))
        if not final:
            for m in range(8):
                zv = xres.v(m * T, (m + 1) * T)
                TS("dve", zv, zv, vcol(gname, m), vcol(bname, m), ALU.mult, ALU.add)

    def ln_final_chunks(gname, bname, tt_out):
        def mk(m):
            def f():
                zv = xres.v(m * T, (m + 1) * T)
                TT("dve", zv, zv, mean_sb.v(), ALU.subtract)
                TT("dve", zv, zv, rstd_sb.v(), ALU.mult)
                ACT(zv, zv, AF.Identity, bias=vcol(bname, m), scale=vcol(gname, m))
                DMA("act", dview(outT[m * 128:(m + 1) * 128, tt_out * T:(tt_out + 1) * T], "d_out"), zv, "o%d" % m)
            return f
        return [mk(m) for m in range(8)]

    def residual_ln(l, which, mm_groups, final):
        gname = "ln_%s_g%d" % (which, l)
        bname = "ln_%s_b%d" % (which, l)
        bm, bq = PBK[6], PBK[7]
        pend = []

        def flush():
            for f in pend:
                f()
            del pend[:]

        for m in range(8):
            combine = mm_groups(m)
            flush()
            zv = xres.v(m * T, (m + 1) * T)
            combine(zv)
            a, b = ln_feed_prep(zv)

            def stats(m=m, a=a, b=b):
                MM(bm.v(), onesB.v(), a, m == 0, m == 7)
                MM(bq.v(), onesB.v(), b, m == 0, m == 7)
            pend.append(stats)
        flush()
        S.phase = S.phase.split(".")[0] + ".ln"
        ln_stats_finish(bm, bq)
        ln_apply(gname, bname, False)

    def mm_kc_outer(groups):
        for kc in range(8):
            for (bv, lf, rf) in groups:
                MM(bv, lf(kc), rf(kc), kc == 0, kc == 7)

    def mixer0(tt, deferred):
        dq = list(deferred) if deferred else []

        def dq_pop(n=1):
            ph = S.phase
            S.phase = "f1.lnapply"
            for _ in range(n):
                if dq:
                    dq.pop(0)()
            S.phase = ph
        S.phase = "m0.win"
        slot = wacq(B_WIN)
        bks = [nb() for _ in range(4)]
        mm_kc_outer([(bks[m].v(), (lambda kc, m=m, slot=slot: slot.v(kc * 512 + m * 128, kc * 512 + (m + 1) * 128)),
                      (lambda kc: xb.v(kc * T, (kc + 1) * T))) for m in range(4)])
        for m in range(4):
            ACT(a_ext.v(m * 527 + 15, m * 527 + 527), bks[m].v(), AF.Identity, scale=1.0 / (2, 4, 8, 16)[m])
            ACT(a_raw.v(m * T, (m + 1) * T), bks[m].v(), AF.Identity)
        wrel()
        dq_pop()
        for blk in range(2):
            slot = wacq(B_WIN + 1 + blk)
            for jj in range(2):
                j = blk * 2 + jj
                bg = nb()
                for kc in range(8):
                    MM(bg.v(), slot.v(kc * 512 + (2 * jj) * 128, kc * 512 + (2 * jj + 1) * 128), xb.v(kc * T, (kc + 1) * T), kc == 0, kc == 7)
                sg = nxt("sig", sig)
                ACT(sg.v(), bg.v(), AF.Sigmoid)
                dq_pop()
                bv = nb()
                for kc in range(8):
                    MM(bv.v(), slot.v(kc * 512 + (2 * jj + 1) * 128, kc * 512 + (2 * jj + 2) * 128), xb.v(kc * T, (kc + 1) * T), kc == 0, kc == 7)
                TT("dve", glu_ext.v(j * 542 + 30, j * 542 + 542), bv.v(), sg.v(), ALU.mult)
            wrel()
        S.phase = "m0.pool"
        for g, w in enumerate((2, 4, 8, 16)):
            A = lambda a, b, g=g: a_ext.v(g * 527 + a, g * 527 + b)
            TT(PCE[0], ptA.v(1, 527), A(1, 527), A(0, 526), ALU.add)
            cur = ptA
            if w >= 4:
                TT(PCE[0], ptB.v(3, 527), ptA.v(3, 527), ptA.v(1, 525), ALU.add)
                cur = ptB
            if w >= 8:
                TT(PCE[0], ptA.v(7, 527), ptB.v(7, 527), ptB.v(3, 523), ALU.add)
                cur = ptA
            if w >= 16:
                TT(PCE[0], ptB.v(15, 527), ptA.v(15, 527), ptA.v(7, 519), ALU.add)
                cur = ptB
            TT(PCE[0], dbuf.v(g * T, (g + 1) * T), cur.v(15, 527), a_raw.v(g * T, (g + 1) * T), ALU.subtract)
            if tt == 0:
                n = w - 1
                TT(PCE[0], tmp16.v(0, n), cur.v(15, 15 + n), cstb.v(144 + g * 16, 144 + g * 16 + n), ALU.mult)
                TT(PCE[0], dbuf.v(g * T, g * T + n), tmp16.v(0, n), a_raw.v(g * T, g * T + n), ALU.subtract)
            CP(PCE[0], A(0, 15), A(512, 527))
        S.phase = "m0.conv"
        bm, bq = PBK[6], PBK[7]
        cpend = []
        for j in range(4):
            bk = nb()
            dslot = wacq(B_DIAG + j)
            for k in range(31):
                MM(bk.v(), dslot.v(k * 128, (k + 1) * 128), glu_ext.v(j * 542 + k, j * 542 + k + 512), k == 0, k == 30)
            wrel()
            for f in cpend:
                f()
            del cpend[:]
            hv = hbuf.v(j * T, (j + 1) * T)
            ACT(hv, bk.v(), AF.Identity, bias=vcol("conv_dw_b", j))
            a, b = ln_feed_prep(hv)

            def cstats(j=j, a=a, b=b):
                MM(bm.v(), onesA.v(), a, j == 0, j == 3)
                MM(bq.v(), onesA.v(), b, j == 0, j == 3)
            cpend.append(cstats)
            CP(PCE[0], glu_ext.v(j * 542, j * 542 + 30), glu_ext.v(j * 542 + 512, j * 542 + 542))
            if j == 0:
                dq_pop(8)
            if j == 1 and deferred:
                load_x(tt, "sp")
        S.phase = "m0.cln"
        for g in range(4):
            bk = nb()
            MM(bk.v(), wsmb.v(g * 128, (g + 1) * 128), dbuf.v(g * T, (g + 1) * T), True, True)
            ACT(ycat.v(g * T, (g + 1) * T), bk.v(), AF.Identity, scale=vcol("pool_scale", g))
        for f in cpend:
            f()
        del cpend[:]
        ln_stats_finish(bm, bq)
        for j in range(4):
            eng = PCE[0] if j == 3 else "dve"
            hv = hbuf.v(j * T, (j + 1) * T)
            TT(eng, hv, hv, mean_sb.v(), ALU.subtract)
            TT(eng, hv, hv, rstd_sb.v(), ALU.mult)
            ACT(ycat.v((4 + j) * T, (5 + j) * T), hv, AF.Silu, bias=vcol("conv_ln_b", j), scale=vcol("conv_ln_g", j))
        S.phase = "m0.wout"
        st = {}

        def groups(m):
            if m % 4 == 0:
                st["slot"] = wacq(B_WOUT + m // 4)
            slot = st["slot"]
            mi = m % 4
            bk = nb()
            for kc in range(8):
                MM(bk.v(), slot.v(kc * 512 + mi * 128, kc * 512 + (mi + 1) * 128), ycat.v(kc * T, (kc + 1) * T), kc == 0, kc == 7)
            if mi == 3:
                wrel()

            def combine(zv):
                STT("dve", zv, zv, ALPHA, bk.v(), ALU.mult, ALU.add)
            return combine
        residual_ln(0, "mix", groups, False)

    def ffn(tt, l, final, hook):
        S.phase = "f%d.up" % l
        b0 = B_FFN0 if l == 0 else B_FFN1
        dwb = VEC_COLS["ffn_dw_w%d" % l]
        for blk in range(11):
            slot = wacq(b0 + blk)
            pre = {}
            if blk == 0:
                grp = []
                for jj in range(2):
                    for vv in range(2):
                        bkx = nb()
                        pre[(jj, vv)] = bkx
                        cb = (2 * jj + vv) * 128
                        grp.append((bkx.v(), (lambda kc, cb=cb, slot=slot: slot.v(kc * 512 + cb, kc * 512 + cb + 128)),
                                    (lambda kc: xb.v(kc * T, (kc + 1) * T))))
                mm_kc_outer(grp)
            for jj in range(2):
                j = blk * 2 + jj
                if blk == 0:
                    bg, bv = pre[(jj, 0)], pre[(jj, 1)]
                else:
                    bg = nb()
                    for kc in range(8):
                        MM(bg.v(), slot.v(kc * 512 + (2 * jj) * 128, kc * 512 + (2 * jj + 1) * 128), xb.v(kc * T, (kc + 1) * T), kc == 0, kc == 7)
                    bv = nb()
                    for kc in range(8):
                        MM(bv.v(), slot.v(kc * 512 + (2 * jj + 1) * 128, kc * 512 + (2 * jj + 2) * 128), xb.v(kc * T, (kc + 1) * T), kc == 0, kc == 7)
                ge = nxt("gext", gext)
                hi = (l * NFF + j) * 2
                CP(PCE[0], ge.v(0, 2), gh.v(hi, hi + 2))
                ACT(ge.v(2, 514), bg.v(), AF.Identity)
                CP(PCE[0], gh.v(hi, hi + 2), ge.v(512, 514))
                c = nxt("ctmp", ctmp)
                TS("dve", c.v(), ge.v(0, 512), vec.v(dwb + j, dwb + j + 1), None, ALU.mult)
                STT("dve", c.v(), ge.v(1, 513), vec.v(dwb + NFF + j, dwb + NFF + j + 1), c.v(), ALU.mult, ALU.add)
                STT("dve", c.v(), ge.v(2, 514), vec.v(dwb + 2 * NFF + j, dwb + 2 * NFF + j + 1), c.v(), ALU.mult, ALU.add)
                gl = nxt("ge", gebuf)
                ACT(gl.v(), c.v(), AF.Gelu_apprx_tanh, bias=vcol("ffn_dw_b%d" % l, j))
                TT("dve", gbuf.v(j * T, (j + 1) * T), bv.v(), gl.v(), ALU.mult)
            wrel()
        S.phase = "f%d.pd" % l
        wpp = lambda kc, m: wsmb.v(512 + l * 2048 + kc * 1024 + m * 128, 512 + l * 2048 + kc * 1024 + (m + 1) * 128)
        bm, bq = PBK[6], PBK[7]
        pend = []

        def flush():
            for f in pend:
                f()
            del pend[:]

        gname = "ln_ffn_g%d" % l
        bname = "ln_ffn_b%d" % l
        for half in range(2):
            pg = wacq(b0 + 11 + half * 5)
            pls = []
            for mi in range(4):
                m = half * 4 + mi
                bgt = nb()
                for kc in range(8):
                    MM(bgt.v(), pg.v(kc * 512 + mi * 128, kc * 512 + (mi + 1) * 128), xb.v(kc * T, (kc + 1) * T), kc == 0, kc == 7)
                sgt = nxt("sgate", sgate)
                ACT(sgt.v(), bgt.v(), AF.Sigmoid, bias=vcol("ple_b_gate%d" % l, m))
                bpp = nb()
                for kc in range(2):
                    MM(bpp.v(), wpp(kc, m), pbuf[l].v(kc * T, (kc + 1) * T), kc == 0, kc == 1)
                pl = nxt("ple", plebuf)
                TT("dve", pl.v(), bpp.v(), sgt.v(), ALU.mult)
                pls.append(pl)
            wrel()
            if half == 1 and hook is not None:
                hook()
            for mi in range(4):
                m = half * 4 + mi
                dslot = wacq(b0 + 12 + half * 5 + mi)
                bd = nb()
                for j in range(NFF):
                    MM(bd.v(), dslot.v(j * 128, (j + 1) * 128), gbuf.v(j * T, (j + 1) * T), j == 0, j == NFF - 1)
                wrel()
                flush()
                zv = xres.v(m * T, (m + 1) * T)
                STT("dve", zv, zv, ALPHA, bd.v(), ALU.mult, ALU.add)
                TT("dve", zv, zv, pls[mi].v(), ALU.add)
                a, b = ln_feed_prep(zv)

                def stats(m=m, a=a, b=b):
                    MM(bm.v(), onesB.v(), a, m == 0, m == 7)
                    MM(bq.v(), onesB.v(), b, m == 0, m == 7)
                pend.append(stats)
        flush()
        S.phase = "f%d.ln" % l
        ln_stats_finish(bm, bq)
        if final:
            return ln_final_chunks(gname, bname, tt)
        ln_apply(gname, bname, False)
        return None

    def mixer1(tt):
        S.phase = "m1.qkv"
        slot_i = tt % 2
        pslot = 1 - slot_i
        for blk in range(2):
            slot = wacq(B_K + blk)
            bks = [nb() for _ in range(4)]
            if blk == 0:
                mm_kc_outer([(bks[mi].v(), (lambda kc, mi=mi, slot=slot: slot.v(kc * 512 + mi * 128, kc * 512 + (mi + 1) * 128)),
                              (lambda kc: xb.v(kc * T, (kc + 1) * T))) for mi in range(4)])
            for mi in range(4):
                m = blk * 4 + mi
                bk = bks[mi]
                if blk != 0:
                    for kc in range(8):
                        MM(bk.v(), slot.v(kc * 512 + mi * 128, kc * 512 + (mi + 1) * 128), xb.v(kc * T, (kc + 1) * T), kc == 0, kc == 7)
                ACT(kT.v((m * 2 + slot_i) * T, (m * 2 + slot_i + 1) * T), bk.v(), AF.Identity)
            wrel()
        for blk in range(2):
            slot = wacq(B_V + blk)
            for tb in range(4):
                bk = nb()
                for kc in range(8):
                    MM(bk.v(), xb.v(kc * T + tb * 128, kc * T + (tb + 1) * 128), slot.v(kc * 512, (kc + 1) * 512), kc == 0, kc == 7)
                vb = (slot_i * 4 + tb) * 1024 + blk * 512
                CP("dve", Vt.v(vb, vb + 512), bk.v())
            wrel()
        for blk in range(2):
            slot = wacq(B_Q + blk)
            for mi in range(4):
                m = blk * 4 + mi
                bk = nb()
                for kc in range(8):
                    MM(bk.v(), slot.v(kc * 512 + mi * 128, kc * 512 + (mi + 1) * 128), xb.v(kc * T, (kc + 1) * T), kc == 0, kc == 7)
                ACT(qT.v(m * T, (m + 1) * T), bk.v(), AF.Identity, scale=0.125)
            wrel()
        S.phase = "m1.att"
        blocks = [("c", 0)]
        if tt > 0:
            blocks += [("p", i) for i in range(4)]
        blocks += [("c", i) for i in range(1, 4)]
        bank_pool[0] = [4, 5, 6, 7]
        pend = []
        nper = len(blocks)

        def flush(keep):
            while len(pend) > keep:
                pend.pop(0)()
        for pair in range(8):
            po = PBK[(pair % 2) * 2]
            pd = PBK[(pair % 2) * 2 + 1]
            for bi, (kind, i) in enumerate(blocks):
                if kind == "p":
                    ks, qlo, N, bc = pslot, 0, (2 + 2 * i) * 64, (8 - 2 * i) * 64
                else:
                    ks, qlo, N, bc = slot_i, 2 * i * 64, (8 - 2 * i) * 64, 0
                kb = (pair * 2 + ks) * T + i * 128
                bss = [nb(), nb()]
                for hp in range(2):
                    MM(bss[hp].v(0, N), kT.v(kb, kb + 128, hp * 64, hp * 64 + 64),
                       qT.v(pair * T + qlo, pair * T + qlo + N, hp * 64, hp * 64 + 64), True, True, skip_group_check=True)
                pts = []
                for hp in range(2):
                    h = pair * 2 + hp
                    pt = nxt("Pt", Pt)
                    ACT(pt.v(0, N), bss[hp].v(0, N), AF.Exp)
                    TT("dve", pt.v(0, N), pt.v(0, N), Bt.v(h * 640 + bc, h * 640 + bc + N), ALU.mult)
                    pts.append(pt)
                first = bi == 0
                last = bi == nper - 1

                def pv(qlo=qlo, N=N, pts=pts, ks=ks, i=i, first=first, last=last, po=po, pd=pd, pair=pair):
                    for hp in range(2):
                        vb = (ks * 4 + i) * 1024 + (pair * 2 + hp) * 64
                        MM(po.v(qlo, qlo + N, hp * 64, hp * 64 + 64), Vt.v(vb, vb + 64), pts[hp].v(0, N), first, last,
                           skip_group_check=True, tile_position=(0, hp * 64))
                    for hp in range(2):
                        MM(pd.v(qlo, qlo + N, hp * 64, hp * 64 + 64), ones64.v(), pts[hp].v(0, N), first, last,
                           skip_group_check=True, tile_position=(0, hp * 64))
                pend.append(pv)
                flush(1)

            def norm(po=po, pd=pd, pair=pair):
                rc = nxt("rec", rec)
                ACT(rc.v(), pd.v(), AF.Ln)
                ACT(rc.v(), rc.v(), AF.Exp, scale=-1.0)
                TT("dve", ao.v(pair * T, (pair + 1) * T), po.v(), rc.v(), ALU.mult)
            pend.append(norm)
        flush(0)
        bank_pool[0] = list(range(6))
        S.phase = "m1.wo"
        st = {}

        def groups(m):
            if m % 4 == 0:
                st["slot"] = wacq(B_WO + m // 4)
            slot = st["slot"]
            mi = m % 4
            bk = nb()
            for kc in range(8):
                MM(bk.v(), slot.v(kc * 512 + mi * 128, kc * 512 + (mi + 1) * 128), ao.v(kc * T, (kc + 1) * T), kc == 0, kc == 7)
            if mi == 3:
                wrel()

            def combine(zv):
                STT("dve", zv, zv, ALPHA, bk.v(), ALU.mult, ALU.add)
            return combine
        residual_ln(1, "mix", groups, False)

    deferred = None
    for tt in range(NT):
        S.phase = "pre"
        PCE[0] = "dve"
        if tt > 0:
            load_p(tt, 0)
            load_p(tt, 1)
        mixer0(tt, deferred)
        ffn(tt, 0, False, None)
        mixer1(tt)
        deferred = ffn(tt, 1, True, (lambda tt=tt: load_xb(tt + 1)) if tt + 1 < NT else None)
    for f in deferred:
        f()
    S.op("pool", None, r=[], w=[xres.v()])
    assert wstate["cur"] == total_loads, (wstate, total_loads)
    S.emit(nc)
    _LAST[0] = S
    return nc


VEC_COLS = {}
BLK_NCOLS = {}


def _vec_layout():
    cols = {}
    c = 0

    def add(name, n):
        nonlocal c
        cols[name] = c
        c += n
    add("pool_scale", 4)
    add("conv_dw_w", 124)
    add("conv_dw_b", 4)
    add("conv_ln_g", 4)
    add("conv_ln_b", 4)
    for l in range(2):
        add("ln_mix_g%d" % l, 8)
        add("ln_mix_b%d" % l, 8)
        add("ffn_dw_w%d" % l, 66)
        add("ffn_dw_b%d" % l, 22)
        add("ple_b_gate%d" % l, 8)
        add("ln_ffn_g%d" % l, 8)
        add("ln_ffn_b%d" % l, 8)
    cols["_total"] = c
    return cols


VEC_COLS.update(_vec_layout())
for _b in range(NBLK):
    BLK_NCOLS[_b] = 4096
for _base in (B_FFN0 + 12, B_FFN0 + 17, B_FFN1 + 12, B_FFN1 + 17):
    for _i in range(4):
        BLK_NCOLS[_base + _i] = 2816
for _i in range(4):
    BLK_NCOLS[B_DIAG + _i] = 3968


def _v(a):
    a = np.asarray(a, np.float32)
    return a.reshape(-1, 128).T


def _fmtA(W):
    return W.reshape(8, 128, 512).transpose(1, 0, 2).reshape(128, 4096)


def _cols(W, starts):
    return np.concatenate([W[:, s:s + 128] for s in starts], axis=1)


def _fmtD(Wd, m):
    a = Wd[:, m * 128:(m + 1) * 128].reshape(NFF, 128, 128).transpose(1, 0, 2).reshape(128, DFF)
    out = np.zeros((128, 4096), np.float32)
    out[:, :DFF] = a
    return out


def _prep_shared(inp):
    f = lambda k: np.asarray(inp[k], np.float32)
    blocks = []
    w_in = f("mix_w_in")[0]
    blocks.append(_fmtA(w_in[:, 0:512]))
    for blk in range(2):
        starts = []
        for jj in range(2):
            j = blk * 2 + jj
            starts += [1024 + j * 128, 512 + j * 128]
        blocks.append(_fmtA(_cols(w_in, starts)))
    for _ in range(4):
        blocks.append(np.zeros((128, 4096), np.float32))
    w_out = f("mix_w_out")[0]
    blocks += [_fmtA(w_out[:, 0:512]), _fmtA(w_out[:, 512:1024])]

    def ffn_blocks(l):
        out = []
        wu = f("ffn_w_up")[l]
        for blk in range(11):
            starts = []
            for jj in range(2):
                j = blk * 2 + jj
                starts += [j * 128, DFF + j * 128]
            out.append(_fmtA(_cols(wu, starts)))
        wg = f("ple_w_gate")[l]
        wd = f("ffn_w_down")[l]
        for half in range(2):
            out.append(_fmtA(wg[:, half * 512:(half + 1) * 512]))
            for mi in range(4):
                out.append(_fmtD(wd, half * 4 + mi))
        return out
    blocks += ffn_blocks(0)
    wqkv = f("attn_w_qkv")[0]
    for s in (1024, 1536, 2048, 2560, 0, 512):
        blocks.append(_fmtA(wqkv[:, s:s + 512]))
    w_o = f("attn_w_o")[0]
    blocks += [_fmtA(w_o[:, 0:512]), _fmtA(w_o[:, 512:1024])]
    blocks += ffn_blocks(1)
    assert len(blocks) == NBLK
    wsrc = np.ascontiguousarray(np.concatenate(blocks, axis=0))

    pw = f("pool_w")[0].transpose(1, 0, 2).reshape(128, 512)
    pps = [f("ple_w_proj")[l].reshape(2, 128, 1024).transpose(1, 0, 2).reshape(128, 2048) for l in range(2)]
    wsm = np.ascontiguousarray(np.concatenate([pw] + pps, axis=1))

    vecs = np.zeros((128, VEC_COLS["_total"]), np.float32)

    def put(name, arr):
        c = VEC_COLS[name]
        vecs[:, c:c + arr.shape[1]] = arr
    put("pool_scale", _v(f("pool_scale")[0]))
    put("conv_dw_w", f("conv_dw_w")[0].reshape(31, 4, 128).transpose(2, 0, 1).reshape(128, 124))
    put("conv_dw_b", _v(f("conv_dw_b")[0]))
    put("conv_ln_g", _v(f("conv_ln_g")[0]))
    put("conv_ln_b", _v(f("conv_ln_b")[0]))
    for l in range(2):
        put("ln_mix_g%d" % l, _v(f("ln_mix_g")[l]))
        put("ln_mix_b%d" % l, _v(f("ln_mix_b")[l]))
        put("ffn_dw_w%d" % l, f("ffn_dw_w")[l].reshape(3, NFF, 128).transpose(2, 0, 1).reshape(128, 66))
        put("ffn_dw_b%d" % l, _v(f("ffn_dw_b")[l]))
        put("ple_b_gate%d" % l, _v(f("ple_b_gate")[l]))
        put("ln_ffn_g%d" % l, _v(f("ln_ffn_g")[l]))
        put("ln_ffn_b%d" % l, _v(f("ln_ffn_b")[l]))

    cst = np.zeros((128, 208), np.float32)
    cst[:, 0:128] = np.eye(128, dtype=np.float32)
    cst[:, 128:144] = (1.0 / np.arange(1, 17, dtype=np.float64)).astype(np.float32)[None, :]
    for _g, _w in enumerate((2, 4, 8, 16)):
        cst[:, 144 + _g * 16:160 + _g * 16] = (float(_w) / np.arange(1, 17, dtype=np.float64)).astype(np.float32)[None, :]

    rb = f("attn_rel_bias")[0]
    ki = np.arange(64)[:, None]
    qi = np.arange(64)[None, :]
    btab = np.full((128, 16, 10, 64), MASKV, np.float32)
    for r in range(10):
        if r <= 8:
            idx = np.clip(64 * r + qi - ki, -256, 256) + 256
            btab[0:64, :, r, :] = rb[:, idx].transpose(1, 0, 2)
        if r >= 1:
            idx = np.clip(64 * (r - 1) + qi - ki, -256, 256) + 256
            btab[64:128, :, r, :] = rb[:, idx].transpose(1, 0, 2)
    btab = np.ascontiguousarray(btab.reshape(128, 16 * 640))
    return dict(wsrc=wsrc, wsm=wsm, vecs=vecs, cst=cst, btab=btab)


_NC_CACHE = {}
_LAST = [None]


def kernel(_nt=None, **inp):
    NT = SEQ // T if _nt is None else _nt
    SC = NT * T
    shared = _prep_shared(inp)
    x = np.asarray(inp["x"], np.float32)
    p = np.asarray(inp["p"], np.float32)
    B = x.shape[0]
    in_maps = []
    for b in range(B):
        m = dict(shared)
        m["xT"] = np.ascontiguousarray(x[b, :SC].T)
        m["pT"] = np.ascontiguousarray(p[:, b, :SC].transpose(0, 2, 1).reshape(512, SC))
        in_maps.append(m)
    if NT not in _NC_CACHE:
        _NC_CACHE[NT] = build(NT)
    nc = _NC_CACHE[NT]
    res = run_bass_kernel_spmd(nc, in_maps, core_ids=list(range(B)))
    out = np.stack([np.asarray(r["outT"], np.float32).T for r in res.results], axis=0)
    return np.ascontiguousarray(out)
```

```python
from contextlib import ExitStack
import numpy as np
import concourse.bass as bass
import concourse.mybir as mybir
from concourse.bass_utils import run_bass_kernel_spmd

F32 = mybir.dt.float32
BF16 = mybir.dt.bfloat16
AF = mybir.ActivationFunctionType
ALU = mybir.AluOpType

D = 1024
SEQ = 4096
T = 512
DFF = 2816
NFF = 22
ALPHA = float(4.0 ** 0.25)
EPS = 1e-5
NBLK = 59
B_WIN, B_DIAG, B_WOUT, B_FFN0, B_K, B_V, B_Q, B_WO, B_FFN1 = 0, 3, 7, 9, 30, 32, 34, 36, 38
RING = 3
MASKV = -30000.0
ARENA_BYTES = 207 * 1024


class View:
    __slots__ = ("ap", "res")

    def __init__(self, ap, res):
        self.ap = ap
        self.res = res


class Buf:
    def __init__(self, ctx, name, off, ncols, dt):
        self.ctx, self.name, self.off, self.n, self.dt = ctx, name, off, ncols, dt
        self.esz = 4 if dt is F32 else 2

    def v(self, a=0, b=None, p0=0, p1=128):
        if b is None:
            b = self.n
        assert 0 <= a < b <= self.n, (self.name, a, b, self.n)
        base = self.ctx["f32"] if self.dt is F32 else self.ctx["bf"]
        e0 = self.off // self.esz
        lo = self.off + a * self.esz
        hi = self.off + b * self.esz
        return View(base[p0:p1, e0 + a:e0 + b], tuple(range(lo // 256, (hi - 1) // 256 + 1)))


class Sched:
    def __init__(self):
        self.ops = []
        self.lastw = {}
        self.lastr = {}
        self.total_keys = set()
        self.phase = ""
        self.phases = []
        self.names = {}
        self.sig_ok = []

    def op(self, eng, fn, r=(), w=(), dma=None, sig_ok=True):
        i = len(self.ops)
        raw, war = set(), set()
        lastw, lastr = self.lastw, self.lastr
        for v in r:
            for c in v.res:
                x = lastw.get(c)
                if x is not None:
                    raw.add(x)
        for v in w:
            for c in v.res:
                x = lastw.get(c)
                if x is not None:
                    raw.add(x)
                d = lastr.get(c)
                if d:
                    war.update(d.values())
        rk = ("dma", i) if dma else eng
        for v in r:
            for c in v.res:
                d = lastr.get(c)
                if d is None:
                    d = lastr[c] = {}
                d[rk] = i
        for v in w:
            for c in v.res:
                lastw[c] = i
                lastr[c] = {}
        raw.discard(i)
        war.discard(i)
        war -= raw
        self.ops.append((eng, fn, raw, war, dma))
        self.sig_ok.append(sig_ok)
        self.phases.append(self.phase)
        return i

    def _needs_wait(self, i, d, kind):
        ei, di = self.ops[i], self.ops[d]
        if di[4] is not None:
            return True
        if ei[0] == di[0]:
            if ei[4] is not None:
                return True
            if ei[0] == "pe":
                return False
            if ei[0] == "pool":
                return True
            return kind == "raw"
        return True

    def emit(self, nc):
        ops = self.ops
        n = len(ops)
        need_sig = [False] * n
        deps = [None] * n
        nxt_ok = [None] * n
        last = {}
        for i in range(n - 1, -1, -1):
            e = ops[i][0]
            if self.sig_ok[i] and ops[i][4] is None and ops[i][1] is not None:
                last[e] = i
            nxt_ok[i] = last.get(e)

        def redirect(i, d):
            if ops[d][4] is not None or self.sig_ok[d]:
                return d
            d2 = nxt_ok[d]
            if d2 is not None and d2 < i:
                return d2
            return d
        for i, o in enumerate(ops):
            dl = []
            for d in o[2]:
                if self._needs_wait(i, d, "raw"):
                    dl.append(redirect(i, d))
            for d in o[3]:
                if self._needs_wait(i, d, "war"):
                    dl.append(redirect(i, d))
            deps[i] = dl
            for d in dl:
                need_sig[d] = True
        cnt = {}
        sig = [None] * n
        for i, o in enumerate(ops):
            if o[4] is not None:
                k = ("dma", o[4])
                cnt[k] = cnt.get(k, 0) + 16
                sig[i] = (k, cnt[k])
            elif need_sig[i]:
                assert o[1] is not None
                k = ("eng", o[0])
                cnt[k] = cnt.get(k, 0) + 1
                sig[i] = (k, cnt[k])
        for i, o in enumerate(ops):
            if o[4] is not None and o[4] in self.total_keys:
                k = ("dma", o[4])
                sig[i] = (k, cnt[k])
        with ExitStack() as st:
            sems = {}
            for k in cnt:
                sems[k] = st.enter_context(nc.semaphore("s_%s_%s" % k))
            block = st.enter_context(nc.Block())
            decos = [("pe", block.tensor), ("act", block.scalar), ("dve", block.vector),
                     ("pool", block.gpsimd), ("sp", block.sync)]
            for engname, deco in decos:
                def body(e, engname=engname):
                    seen = {}
                    for i, o in enumerate(ops):
                        if o[0] != engname:
                            continue
                        waits = {}
                        for d in deps[i]:
                            k, c = sig[d]
                            if waits.get(k, 0) < c:
                                waits[k] = c
                        for k, c in waits.items():
                            if seen.get(k, 0) < c:
                                e.wait_ge(sems[k], c)
                                seen[k] = c
                        if o[1] is not None:
                            try:
                                self.names[i] = nc.get_next_instruction_name()
                            except Exception:
                                pass
                            ins = o[1](e)
                            if sig[i] is not None:
                                ins.then_inc(sems[sig[i][0]], 16 if o[4] is not None else 1)
                deco(body)
        return cnt


def build(NT):
    nc = bass.Bass("TRN2", target_bir_lowering=False)
    SC = NT * T
    NV = VEC_COLS["_total"]
    xT = nc.dram_tensor("xT", [D, SC], F32, kind="ExternalInput").ap()
    pT = nc.dram_tensor("pT", [2 * 256, SC], F32, kind="ExternalInput").ap()
    wsrc = nc.dram_tensor("wsrc", [NBLK * 128, 4096], F32, kind="ExternalInput").ap()
    wsm = nc.dram_tensor("wsm", [128, 512 + 4096], F32, kind="ExternalInput").ap()
    vecs = nc.dram_tensor("vecs", [128, NV], F32, kind="ExternalInput").ap()
    cst = nc.dram_tensor("cst", [128, 208], F32, kind="ExternalInput").ap()
    btab = nc.dram_tensor("btab", [128, 16 * 640], F32, kind="ExternalInput").ap()
    outT = nc.dram_tensor("outT", [D, SC], F32, kind="ExternalOutput").ap()
    wc = nc.dram_tensor("wc", [NBLK * 128, 4096], BF16).ap()

    arena = nc.alloc_sbuf_tensor("arena", [128, ARENA_BYTES // 4], F32)
    ctx = {"f32": arena, "bf": arena.bitcast(BF16)}
    ps_t = [nc.alloc_psum_tensor("psb%d" % i, [128, 512], F32) for i in range(8)]

    class PB:
        def __init__(self, i):
            self.i = i

        def v(self, a=0, b=512, p0=0, p1=128):
            return View(ps_t[self.i][p0:p1, a:b], ("ps%d" % self.i,))

    PBK = [PB(i) for i in range(8)]
    S = Sched()
    off = [0]

    def alloc(name, ncols, dt):
        esz = 4 if dt is F32 else 2
        b = Buf(ctx, name, off[0], ncols, dt)
        off[0] += (ncols * esz + 255) // 256 * 256
        assert off[0] <= ARENA_BYTES, (name, off[0])
        return b

    cstb = alloc("cst", 208, F32)
    identb = alloc("identb", 128, BF16)
    vec = alloc("vec", NV, F32)
    onesA = alloc("onesA", 128, BF16)
    onesB = alloc("onesB", 128, BF16)
    ones64 = alloc("ones64", 64, BF16)
    epsb = alloc("eps", 2, F32)
    wsmb = alloc("wsmb", 512 + 4096, BF16)
    Bt = alloc("Bt", 16 * 640, BF16)
    xres = alloc("xres", 8 * T, F32)
    xb = alloc("xb", 8 * T, BF16)
    a_ext = alloc("a_ext", 4 * 527, F32)
    glu_ext = alloc("glu_ext", 4 * 542, BF16)
    gh = alloc("gh", 2 * NFF * 2, F32)
    kT = alloc("kT", 8 * 2 * T, BF16)
    Vt = alloc("Vt", 8 * 1024, BF16)
    pbuf = [alloc("p%d" % l, 2 * T, BF16) for l in range(2)]
    ring = [alloc("ring%d" % i, 4096, BF16) for i in range(RING)]
    mean_sb = alloc("mean_sb", T, F32)
    var_sb = alloc("var_sb", T, F32)
    rstd_sb = alloc("rstd_sb", T, F32)
    zb = [alloc("zb%d" % i, T, BF16) for i in range(2)]
    z2b = [alloc("z2b%d" % i, T, BF16) for i in range(2)]
    tmp16 = alloc("tmp16", 16, F32)
    base = off[0]
    ptA = alloc("ptA", 527, F32)
    ptB = alloc("ptB", 527, F32)
    dbuf = alloc("dbuf", 4 * T, BF16)
    sig = [alloc("sig%d" % i, T, F32) for i in range(2)]
    hbuf = alloc("h", 4 * T, F32)
    a_raw = alloc("a_raw", 4 * T, F32)
    ycat = alloc("ycat", 8 * T, BF16)
    end0 = off[0]
    off[0] = base
    gbuf = alloc("g", NFF * T, BF16)
    gext = [alloc("gext%d" % i, 514, F32) for i in range(2)]
    ctmp = [alloc("ctmp%d" % i, T, F32) for i in range(2)]
    gebuf = [alloc("ge%d" % i, T, F32) for i in range(2)]
    sgate = [alloc("sgate%d" % i, T, F32) for i in range(2)]
    plebuf = [alloc("ple%d" % i, T, F32) for i in range(4)]
    end1 = off[0]
    off[0] = base
    qT = alloc("qT", 8 * T, BF16)
    ao = alloc("ao", 8 * T, BF16)
    Pt = [alloc("Pt%d" % i, T, BF16) for i in range(6)]
    rec = [alloc("rec%d" % i, T, F32) for i in range(2)]
    end2 = off[0]
    off[0] = max(end0, end1, end2)
    assert off[0] <= ARENA_BYTES, off[0]

    PCE = ["pool"]
    rot = {}

    def nxt(name, lst):
        i = rot.get(name, 0)
        rot[name] = i + 1
        return lst[i % len(lst)]

    bank_ctr = [0]
    bank_pool = [list(range(6))]

    def nb():
        pool = bank_pool[0]
        b = pool[bank_ctr[0] % len(pool)]
        bank_ctr[0] += 1
        return PBK[b]

    def dview(ap, name):
        return View(ap, (name,))

    def MM(out, lhsT, rhs, start, stop, **kw):
        S.op("pe", lambda e: e.matmul(out.ap, lhsT=lhsT.ap, rhs=rhs.ap, start=start, stop=stop, **kw),
             r=[lhsT, rhs], w=[out], sig_ok=bool(stop))

    zero_col = epsb.v(1, 2)

    def ACT(out, in_, func, bias=None, scale=1.0):
        rr = [in_]
        if bias is None:
            bias = zero_col
        rr.append(bias)
        if isinstance(scale, View):
            rr.append(scale)
            sc = scale.ap
        else:
            sc = scale
        S.op("act", lambda e: e.activation(out=out.ap, in_=in_.ap, func=func, bias=bias.ap, scale=sc),
             r=rr, w=[out])

    def TT(eng, out, a, b, op):
        S.op(eng, lambda e: e.tensor_tensor(out=out.ap, in0=a.ap, in1=b.ap, op=op), r=[a, b], w=[out])

    def STT(eng, out, in0, scalar, in1, op0, op1):
        rr = [in0, in1]
        if isinstance(scalar, View):
            rr.append(scalar)
            sc = scalar.ap
        else:
            sc = scalar
        S.op(eng, lambda e: e.scalar_tensor_tensor(out=out.ap, in0=in0.ap, scalar=sc, in1=in1.ap, op0=op0, op1=op1),
             r=rr, w=[out])

    def TS(eng, out, in0, s1, s2, op0, op1=None):
        rr = [in0]
        a1 = s1
        a2 = s2
        if isinstance(s1, View):
            rr.append(s1)
            a1 = s1.ap
        if isinstance(s2, View):
            rr.append(s2)
            a2 = s2.ap
        if op1 is None:
            S.op(eng, lambda e: e.tensor_scalar(out=out.ap, in0=in0.ap, scalar1=a1, scalar2=None, op0=op0),
                 r=rr, w=[out])
        else:
            S.op(eng, lambda e: e.tensor_scalar(out=out.ap, in0=in0.ap, scalar1=a1, scalar2=a2, op0=op0, op1=op1),
                 r=rr, w=[out])

    def CP(eng, out, in_):
        S.op(eng, lambda e: e.tensor_copy(out=out.ap, in_=in_.ap), r=[in_], w=[out])

    def MSET(eng, out, val):
        S.op(eng, lambda e: e.memset(out.ap, val), r=[], w=[out])

    def RCP(out, in_):
        S.op("dve", lambda e: e.reciprocal(out=out.ap, in_=in_.ap), r=[in_], w=[out])

    def DMA(eng, out, in_, key):
        S.op(eng, lambda e: e.dma_start(out=out.ap, in_=in_.ap), r=[in_], w=[out], dma=key)

    def vcol(name, c=0):
        b0 = VEC_COLS[name]
        return vec.v(b0 + c, b0 + c + 1)

    def load_x(tt, xq="pool"):
        for kc in range(8):
            DMA(xq, xres.v(kc * T, (kc + 1) * T),
                dview(xT[kc * 128:(kc + 1) * 128, tt * T:(tt + 1) * T], "d_x"), ("xs%d" if xq == "sp" else "x%d") % kc)

    def load_xb(tt):
        for kc in range(8):
            DMA("pool", xb.v(kc * T, (kc + 1) * T),
                dview(xT[kc * 128:(kc + 1) * 128, tt * T:(tt + 1) * T], "d_x"), "xb%d" % kc)

    def load_p(tt, l):
        for kc in range(2):
            DMA("pool", pbuf[l].v(kc * T, (kc + 1) * T),
                dview(pT[l * 256 + kc * 128:l * 256 + (kc + 1) * 128, tt * T:(tt + 1) * T], "d_p"), "p%d_%d" % (l, kc))

    DMA("pool", cstb.v(), dview(cst, "d_cst"), "c0")
    DMA("pool", identb.v(), dview(cst[:, 0:128], "d_cst"), "c1")
    DMA("pool", vec.v(), dview(vecs, "d_vecs"), "c2")
    DMA("pool", wsmb.v(), dview(wsm, "d_wsm"), "c4")
    load_xb(0)
    load_x(0)
    load_p(0, 0)
    load_p(0, 1)
    ncols_blk = [BLK_NCOLS[b] for b in range(NBLK)]
    CVLA = NBLK
    cvstate = {"n": 0}

    def emit_conv_upto(bmax):
        while cvstate["n"] <= min(bmax, NBLK - 1):
            b = cvstate["n"]
            cvstate["n"] += 1
            if B_DIAG <= b < B_DIAG + 4:
                continue
            ncb = ncols_blk[b]
            DMA("pool", dview(wc[b * 128:(b + 1) * 128, 0:ncb], "wc%d" % b),
                dview(wsrc[b * 128:(b + 1) * 128, 0:ncb], "d_wsrc"), "cv%d" % b)
    MSET("dve", onesA.v(), 1.0 / 512)
    MSET("dve", onesB.v(), 1.0 / 1024)
    MSET("dve", ones64.v(), 1.0)
    MSET("dve", epsb.v(0, 1), EPS)
    MSET("dve", epsb.v(1, 2), 0.0)
    MSET("dve", a_ext.v(), 0.0)
    MSET("dve", glu_ext.v(), 0.0)
    MSET("dve", gh.v(), 0.0)
    for c5 in range(5):
        DMA("pool", hbuf.v(), dview(btab[:, c5 * 2048:(c5 + 1) * 2048], "d_btab"), "c3")
        ACT(Bt.v(c5 * 2048, (c5 + 1) * 2048), hbuf.v(), AF.Exp)
    identf = cstb.v(0, 128)
    invc = lambda a, b: cstb.v(128 + a, 128 + b)
    cwb = VEC_COLS["conv_dw_w"]
    PCE[0] = "dve"
    for j in range(4):
        gb = (j % 2) * 4096
        for k in range(31):
            i = k * 4 + j
            TS(PCE[0], gbuf.v(gb + k * 128, gb + (k + 1) * 128), identf, vec.v(cwb + i, cwb + i + 1), None, ALU.mult)
        b = B_DIAG + j
        DMA("sp", dview(wc[b * 128:(b + 1) * 128, 0:3968], "wc%d" % b), gbuf.v(gb, gb + 3968), "dg%d" % (j % 2))

    total_loads = NT * NBLK
    wstate = {"emitted": 0, "cur": 0}

    def emit_load(L):
        b = L % NBLK
        slot = ring[L % RING]
        ncb = ncols_blk[b]
        if L < NBLK and not (B_DIAG <= b < B_DIAG + 4):
            S.op("pool", lambda e: e.dma_start(out=slot.v(0, ncb).ap, in_=wsrc[b * 128:(b + 1) * 128, 0:ncb]),
                 r=[View(None, ("d_wsrc",))], w=[slot.v(0, ncb)], dma="ringp%d" % (L % RING))
            S.op("sp", lambda e: e.dma_start(out=wc[b * 128:(b + 1) * 128, 0:ncb], in_=slot.v(0, ncb).ap),
                 r=[slot.v(0, ncb)], w=[View(None, ("wc%d" % b,))], dma="wb%d" % (L % RING))
            return
        S.op("sp", lambda e: e.dma_start(out=slot.v(0, ncb).ap, in_=wc[b * 128:(b + 1) * 128, 0:ncb]),
             r=[View(None, ("wc%d" % b,))], w=[slot.v(0, ncb)], dma="ring%d" % (L % RING))

    def wacq(expect_b):
        L = wstate["cur"]
        assert L % NBLK == expect_b, (L, expect_b)
        while wstate["emitted"] <= L:
            emit_load(wstate["emitted"])
            wstate["emitted"] += 1
        return ring[L % RING]

    def wrel():
        L = wstate["cur"]
        wstate["cur"] = L + 1
        nl = L + RING
        if nl < total_loads and wstate["emitted"] <= nl:
            while wstate["emitted"] <= nl:
                emit_load(wstate["emitted"])
                wstate["emitted"] += 1

    for L in range(min(RING, total_loads)):
        emit_load(L)
        wstate["emitted"] += 1

    def ln_feed_prep(zv):
        a = nxt("zb", zb)
        b = nxt("z2b", z2b)
        ACT(a.v(), zv, AF.Identity)
        ACT(b.v(), zv, AF.Square)
        return a.v(), b.v()

    def ln_stats_finish(bm, bq):
        ACT(var_sb.v(), bm.v(), AF.Square)
        ACT(mean_sb.v(), bm.v(), AF.Identity)
        STT("dve", var_sb.v(), bq.v(), EPS, var_sb.v(), ALU.add, ALU.subtract)
        ACT(var_sb.v(), var_sb.v(), AF.Ln)
        ACT(rstd_sb.v(), var_sb.v(), AF.Exp, scale=-0.5)

    POOL_CHUNKS = ()

    def ln_apply(gname, bname, final, tt_out=None):
        dch = [m for m in range(8) if m not in POOL_CHUNKS]
        subbed = set()

        def sub(m):
            eng = PCE[0] if m in POOL_CHUNKS else "dve"
            zv = xres.v(m * T, (m + 1) * T)
            TT(eng, zv, zv, mean_sb.v(), ALU.subtract)
            subbed.add(m)
        for m in dch[:2]:
            sub(m)
        for m in range(8):
            eng = PCE[0] if m in POOL_CHUNKS else "dve"
            zv = xres.v(m * T, (m + 1) * T)
            if m not in subbed:
                sub(m)
            TT(eng, zv, zv, rstd_sb.v(), ALU.mult)
            if m in dch:
                i = dch.index(m)
                if i + 2 < len(dch) and dch[i + 2] not in subbed:
                    sub(dch[i + 2])
            if final:
                ACT(zv, zv, AF.Identity, bias=vcol(bname, m), scale=vcol(gname, m))
                DMA("act", dview(outT[m * 128:(m + 1) * 128, tt_out * T:(tt_out + 1) * T], "d_out"), zv, "o%d" % m)
            else:
                ACT(xb.v(m * T, (m + 1) * T), zv, AF.Identity, bias=vcol(bname, m), scale=vcol(gname, m))
        if not final:
            for m in range(8):
                zv = xres.v(m * T, (m + 1) * T)
                TS("dve", zv, zv, vcol(gname, m), vcol(bname, m), ALU.mult, ALU.add)

    def ln_final_chunks(gname, bname, tt_out):
        def mk(m):
            def f():
                zv = xres.v(m * T, (m + 1) * T)
                TT("dve", zv, zv, mean_sb.v(), ALU.subtract)
                TT("dve", zv, zv, rstd_sb.v(), ALU.mult)
                ACT(zv, zv, AF.Identity, bias=vcol(bname, m), scale=vcol(gname, m))
                DMA("act", dview(outT[m * 128:(m + 1) * 128, tt_out * T:(tt_out + 1) * T], "d_out"), zv, "o%d" % m)
            return f
        return [mk(m) for m in range(8)]

    def residual_ln(l, which, mm_groups, final):
        gname = "ln_%s_g%d" % (which, l)
        bname = "ln_%s_b%d" % (which, l)
        bm, bq = PBK[6], PBK[7]
        pend = []

        def flush():
            for f in pend:
                f()
            del pend[:]

        for m in range(8):
            combine = mm_groups(m)
            flush()
            zv = xres.v(m * T, (m + 1) * T)
            combine(zv)
            a, b = ln_feed_prep(zv)

            def stats(m=m, a=a, b=b):
                MM(bm.v(), onesB.v(), a, m == 0, m == 7)
                MM(bq.v(), onesB.v(), b, m == 0, m == 7)
            pend.append(stats)
        flush()
        S.phase = S.phase.split(".")[0] + ".ln"
        ln_stats_finish(bm, bq)
        ln_apply(gname, bname, False)

    def mm_kc_outer(groups):
        for kc in range(8):
            for (bv, lf, rf) in groups:
                MM(bv, lf(kc), rf(kc), kc == 0, kc == 7)

    def mixer0(tt, deferred):
        dq = list(deferred) if deferred else []

        def dq_pop(n=1):
            ph = S.phase
            S.phase = "f1.lnapply"
            for _ in range(n):
                if dq:
                    dq.pop(0)()
            S.phase = ph
        S.phase = "m0.win"
        slot = wacq(B_WIN)
        bks = [nb() for _ in range(4)]
        mm_kc_outer([(bks[m].v(), (lambda kc, m=m, slot=slot: slot.v(kc * 512 + m * 128, kc * 512 + (m + 1) * 128)),
                      (lambda kc: xb.v(kc * T, (kc + 1) * T))) for m in range(4)])
        for m in range(4):
            ACT(a_ext.v(m * 527 + 15, m * 527 + 527), bks[m].v(), AF.Identity, scale=1.0 / (2, 4, 8, 16)[m])
            ACT(a_raw.v(m * T, (m + 1) * T), bks[m].v(), AF.Identity)
        wrel()
        dq_pop()
        for blk in range(2):
            slot = wacq(B_WIN + 1 + blk)
            for jj in range(2):
                j = blk * 2 + jj
                bg = nb()
                for kc in range(8):
                    MM(bg.v(), slot.v(kc * 512 + (2 * jj) * 128, kc * 512 + (2 * jj + 1) * 128), xb.v(kc * T, (kc + 1) * T), kc == 0, kc == 7)
                sg = nxt("sig", sig)
                ACT(sg.v(), bg.v(), AF.Sigmoid)
                dq_pop()
                bv = nb()
                for kc in range(8):
                    MM(bv.v(), slot.v(kc * 512 + (2 * jj + 1) * 128, kc * 512 + (2 * jj + 2) * 128), xb.v(kc * T, (kc + 1) * T), kc == 0, kc == 7)
                TT("dve", glu_ext.v(j * 542 + 30, j * 542 + 542), bv.v(), sg.v(), ALU.mult)
            wrel()
        S.phase = "m0.pool"
        for g, w in enumerate((2, 4, 8, 16)):
            A = lambda a, b, g=g: a_ext.v(g * 527 + a, g * 527 + b)
            TT(PCE[0], ptA.v(1, 527), A(1, 527), A(0, 526), ALU.add)
            cur = ptA
            if w >= 4:
                TT(PCE[0], ptB.v(3, 527), ptA.v(3, 527), ptA.v(1, 525), ALU.add)
                cur = ptB
            if w >= 8:
                TT(PCE[0], ptA.v(7, 527), ptB.v(7, 527), ptB.v(3, 523), ALU.add)
                cur = ptA
            if w >= 16:
                TT(PCE[0], ptB.v(15, 527), ptA.v(15, 527), ptA.v(7, 519), ALU.add)
                cur = ptB
            TT(PCE[0], dbuf.v(g * T, (g + 1) * T), cur.v(15, 527), a_raw.v(g * T, (g + 1) * T), ALU.subtract)
            if tt == 0:
                n = w - 1
                TT(PCE[0], tmp16.v(0, n), cur.v(15, 15 + n), cstb.v(144 + g * 16, 144 + g * 16 + n), ALU.mult)
                TT(PCE[0], dbuf.v(g * T, g * T + n), tmp16.v(0, n), a_raw.v(g * T, g * T + n), ALU.subtract)
            CP(PCE[0], A(0, 15), A(512, 527))
        S.phase = "m0.conv"
        bm, bq = PBK[6], PBK[7]
        cpend = []
        for j in range(4):
            bk = nb()
            dslot = wacq(B_DIAG + j)
            for k in range(31):
                MM(bk.v(), dslot.v(k * 128, (k + 1) * 128), glu_ext.v(j * 542 + k, j * 542 + k + 512), k == 0, k == 30)
            wrel()
            for f in cpend:
                f()
            del cpend[:]
            hv = hbuf.v(j * T, (j + 1) * T)
            ACT(hv, bk.v(), AF.Identity, bias=vcol("conv_dw_b", j))
            a, b = ln_feed_prep(hv)

            def cstats(j=j, a=a, b=b):
                MM(bm.v(), onesA.v(), a, j == 0, j == 3)
                MM(bq.v(), onesA.v(), b, j == 0, j == 3)
            cpend.append(cstats)
            CP(PCE[0], glu_ext.v(j * 542, j * 542 + 30), glu_ext.v(j * 542 + 512, j * 542 + 542))
            if j == 0:
                dq_pop(8)
            if j == 1 and deferred:
                load_x(tt, "sp")
        S.phase = "m0.cln"
        for g in range(4):
            bk = nb()
            MM(bk.v(), wsmb.v(g * 128, (g + 1) * 128), dbuf.v(g * T, (g + 1) * T), True, True)
            ACT(ycat.v(g * T, (g + 1) * T), bk.v(), AF.Identity, scale=vcol("pool_scale", g))
        for f in cpend:
            f()
        del cpend[:]
        ln_stats_finish(bm, bq)
        for j in range(4):
            eng = "dve"
            hv = hbuf.v(j * T, (j + 1) * T)
            TT(eng, hv, hv, mean_sb.v(), ALU.subtract)
            TT(eng, hv, hv, rstd_sb.v(), ALU.mult)
            ACT(ycat.v((4 + j) * T, (5 + j) * T), hv, AF.Silu, bias=vcol("conv_ln_b", j), scale=vcol("conv_ln_g", j))
        S.phase = "m0.wout"
        st = {}

        def groups(m):
            if m % 4 == 0:
                st["slot"] = wacq(B_WOUT + m // 4)
            slot = st["slot"]
            mi = m % 4
            bk = nb()
            for kc in range(8):
                MM(bk.v(), slot.v(kc * 512 + mi * 128, kc * 512 + (mi + 1) * 128), ycat.v(kc * T, (kc + 1) * T), kc == 0, kc == 7)
            if mi == 3:
                wrel()

            def combine(zv):
                STT("dve", zv, zv, ALPHA, bk.v(), ALU.mult, ALU.add)
            return combine
        residual_ln(0, "mix", groups, False)

    def ffn(tt, l, final, hook):
        S.phase = "f%d.up" % l
        b0 = B_FFN0 if l == 0 else B_FFN1
        dwb = VEC_COLS["ffn_dw_w%d" % l]
        for blk in range(11):
            slot = wacq(b0 + blk)
            pre = {}
            if blk == 0:
                grp = []
                for jj in range(2):
                    for vv in range(2):
                        bkx = nb()
                        pre[(jj, vv)] = bkx
                        cb = (2 * jj + vv) * 128
                        grp.append((bkx.v(), (lambda kc, cb=cb, slot=slot: slot.v(kc * 512 + cb, kc * 512 + cb + 128)),
                                    (lambda kc: xb.v(kc * T, (kc + 1) * T))))
                mm_kc_outer(grp)
            for jj in range(2):
                j = blk * 2 + jj
                if blk == 0:
                    bg, bv = pre[(jj, 0)], pre[(jj, 1)]
                else:
                    bg = nb()
                    for kc in range(8):
                        MM(bg.v(), slot.v(kc * 512 + (2 * jj) * 128, kc * 512 + (2 * jj + 1) * 128), xb.v(kc * T, (kc + 1) * T), kc == 0, kc == 7)
                    bv = nb()
                    for kc in range(8):
                        MM(bv.v(), slot.v(kc * 512 + (2 * jj + 1) * 128, kc * 512 + (2 * jj + 2) * 128), xb.v(kc * T, (kc + 1) * T), kc == 0, kc == 7)
                ge = nxt("gext", gext)
                hi = (l * NFF + j) * 2
                CP(PCE[0], ge.v(0, 2), gh.v(hi, hi + 2))
                ACT(ge.v(2, 514), bg.v(), AF.Identity)
                CP(PCE[0], gh.v(hi, hi + 2), ge.v(512, 514))
                c = nxt("ctmp", ctmp)
                TS("dve", c.v(), ge.v(0, 512), vec.v(dwb + j, dwb + j + 1), None, ALU.mult)
                STT("dve", c.v(), ge.v(1, 513), vec.v(dwb + NFF + j, dwb + NFF + j + 1), c.v(), ALU.mult, ALU.add)
                STT("dve", c.v(), ge.v(2, 514), vec.v(dwb + 2 * NFF + j, dwb + 2 * NFF + j + 1), c.v(), ALU.mult, ALU.add)
                gl = nxt("ge", gebuf)
                ACT(gl.v(), c.v(), AF.Gelu_apprx_tanh, bias=vcol("ffn_dw_b%d" % l, j))
                TT("dve", gbuf.v(j * T, (j + 1) * T), bv.v(), gl.v(), ALU.mult)
            wrel()
        S.phase = "f%d.pd" % l
        wpp = lambda kc, m: wsmb.v(512 + l * 2048 + kc * 1024 + m * 128, 512 + l * 2048 + kc * 1024 + (m + 1) * 128)
        bm, bq = PBK[6], PBK[7]
        pend = []

        def flush():
            for f in pend:
                f()
            del pend[:]

        gname = "ln_ffn_g%d" % l
        bname = "ln_ffn_b%d" % l
        for half in range(2):
            pg = wacq(b0 + 11 + half * 5)
            pls = []
            for mi in range(4):
                m = half * 4 + mi
                bgt = nb()
                for kc in range(8):
                    MM(bgt.v(), pg.v(kc * 512 + mi * 128, kc * 512 + (mi + 1) * 128), xb.v(kc * T, (kc + 1) * T), kc == 0, kc == 7)
                sgt = nxt("sgate", sgate)
                ACT(sgt.v(), bgt.v(), AF.Sigmoid, bias=vcol("ple_b_gate%d" % l, m))
                bpp = nb()
                for kc in range(2):
                    MM(bpp.v(), wpp(kc, m), pbuf[l].v(kc * T, (kc + 1) * T), kc == 0, kc == 1)
                pl = nxt("ple", plebuf)
                TT("dve", pl.v(), bpp.v(), sgt.v(), ALU.mult)
                pls.append(pl)
            wrel()
            if half == 1 and hook is not None:
                hook()
            for mi in range(4):
                m = half * 4 + mi
                dslot = wacq(b0 + 12 + half * 5 + mi)
                bd = nb()
                for j in range(NFF):
                    MM(bd.v(), dslot.v(j * 128, (j + 1) * 128), gbuf.v(j * T, (j + 1) * T), j == 0, j == NFF - 1)
                wrel()
                flush()
                zv = xres.v(m * T, (m + 1) * T)
                STT("dve", zv, zv, ALPHA, bd.v(), ALU.mult, ALU.add)
                TT("dve", zv, zv, pls[mi].v(), ALU.add)
                a, b = ln_feed_prep(zv)

                def stats(m=m, a=a, b=b):
                    MM(bm.v(), onesB.v(), a, m == 0, m == 7)
                    MM(bq.v(), onesB.v(), b, m == 0, m == 7)
                pend.append(stats)
        flush()
        S.phase = "f%d.ln" % l
        ln_stats_finish(bm, bq)
        if final:
            return ln_final_chunks(gname, bname, tt)
        ln_apply(gname, bname, False)
        return None

    def mixer1(tt):
        S.phase = "m1.qkv"
        slot_i = tt % 2
        pslot = 1 - slot_i
        for blk in range(2):
            slot = wacq(B_K + blk)
            bks = [nb() for _ in range(4)]
            if blk == 0:
                mm_kc_outer([(bks[mi].v(), (lambda kc, mi=mi, slot=slot: slot.v(kc * 512 + mi * 128, kc * 512 + (mi + 1) * 128)),
                              (lambda kc: xb.v(kc * T, (kc + 1) * T))) for mi in range(4)])
            for mi in range(4):
                m = blk * 4 + mi
                bk = bks[mi]
                if blk != 0:
                    for kc in range(8):
                        MM(bk.v(), slot.v(kc * 512 + mi * 128, kc * 512 + (mi + 1) * 128), xb.v(kc * T, (kc + 1) * T), kc == 0, kc == 7)
                ACT(kT.v((m * 2 + slot_i) * T, (m * 2 + slot_i + 1) * T), bk.v(), AF.Identity)
            wrel()
        for blk in range(2):
            slot = wacq(B_V + blk)
            for tb in range(4):
                bk = nb()
                for kc in range(8):
                    MM(bk.v(), xb.v(kc * T + tb * 128, kc * T + (tb + 1) * 128), slot.v(kc * 512, (kc + 1) * 512), kc == 0, kc == 7)
                vb = (slot_i * 4 + tb) * 1024 + blk * 512
                CP("dve", Vt.v(vb, vb + 512), bk.v())
            wrel()
        for blk in range(2):
            slot = wacq(B_Q + blk)
            for mi in range(4):
                m = blk * 4 + mi
                bk = nb()
                for kc in range(8):
                    MM(bk.v(), slot.v(kc * 512 + mi * 128, kc * 512 + (mi + 1) * 128), xb.v(kc * T, (kc + 1) * T), kc == 0, kc == 7)
                ACT(qT.v(m * T, (m + 1) * T), bk.v(), AF.Identity, scale=0.125)
            wrel()
        S.phase = "m1.att"
        blocks = [("c", 0)]
        if tt > 0:
            blocks += [("p", i) for i in range(4)]
        blocks += [("c", i) for i in range(1, 4)]
        bank_pool[0] = [4, 5, 6, 7]
        pend = []
        nper = len(blocks)

        def flush(keep):
            while len(pend) > keep:
                pend.pop(0)()
        for pair in range(8):
            po = PBK[(pair % 2) * 2]
            pd = PBK[(pair % 2) * 2 + 1]
            for bi, (kind, i) in enumerate(blocks):
                if kind == "p":
                    ks, qlo, N, bc = pslot, 0, (2 + 2 * i) * 64, (8 - 2 * i) * 64
                else:
                    ks, qlo, N, bc = slot_i, 2 * i * 64, (8 - 2 * i) * 64, 0
                kb = (pair * 2 + ks) * T + i * 128
                bss = [nb(), nb()]
                for hp in range(2):
                    MM(bss[hp].v(0, N), kT.v(kb, kb + 128, hp * 64, hp * 64 + 64),
                       qT.v(pair * T + qlo, pair * T + qlo + N, hp * 64, hp * 64 + 64), True, True, skip_group_check=True)
                pts = []
                for hp in range(2):
                    h = pair * 2 + hp
                    pt = nxt("Pt", Pt)
                    ACT(pt.v(0, N), bss[hp].v(0, N), AF.Exp)
                    TT("dve", pt.v(0, N), pt.v(0, N), Bt.v(h * 640 + bc, h * 640 + bc + N), ALU.mult)
                    pts.append(pt)
                first = bi == 0
                last = bi == nper - 1

                def pv(qlo=qlo, N=N, pts=pts, ks=ks, i=i, first=first, last=last, po=po, pd=pd, pair=pair):
                    for hp in range(2):
                        vb = (ks * 4 + i) * 1024 + (pair * 2 + hp) * 64
                        MM(po.v(qlo, qlo + N, hp * 64, hp * 64 + 64), Vt.v(vb, vb + 64), pts[hp].v(0, N), first, last,
                           skip_group_check=True, tile_position=(0, hp * 64))
                    for hp in range(2):
                        MM(pd.v(qlo, qlo + N, hp * 64, hp * 64 + 64), ones64.v(), pts[hp].v(0, N), first, last,
                           skip_group_check=True, tile_position=(0, hp * 64))
                pend.append(pv)
                flush(1)

            def norm(po=po, pd=pd, pair=pair):
                rc = nxt("rec", rec)
                ACT(rc.v(), pd.v(), AF.Ln)
                ACT(rc.v(), rc.v(), AF.Exp, scale=-1.0)
                TT("dve", ao.v(pair * T, (pair + 1) * T), po.v(), rc.v(), ALU.mult)
            pend.append(norm)
        flush(0)
        bank_pool[0] = list(range(6))
        S.phase = "m1.wo"
        st = {}

        def groups(m):
            if m % 4 == 0:
                st["slot"] = wacq(B_WO + m // 4)
            slot = st["slot"]
            mi = m % 4
            bk = nb()
            for kc in range(8):
                MM(bk.v(), slot.v(kc * 512 + mi * 128, kc * 512 + (mi + 1) * 128), ao.v(kc * T, (kc + 1) * T), kc == 0, kc == 7)
            if mi == 3:
                wrel()

            def combine(zv):
                STT("dve", zv, zv, ALPHA, bk.v(), ALU.mult, ALU.add)
            return combine
        residual_ln(1, "mix", groups, False)

    deferred = None
    for tt in range(NT):
        S.phase = "pre"
        PCE[0] = "dve" if tt == 0 else "pool"
        if tt > 0:
            load_p(tt, 0)
            load_p(tt, 1)
        mixer0(tt, deferred)
        ffn(tt, 0, False, None)
        mixer1(tt)
        deferred = ffn(tt, 1, True, (lambda tt=tt: load_xb(tt + 1)) if tt + 1 < NT else None)
    for f in deferred:
        f()
    S.op("pool", None, r=[], w=[xres.v()])
    assert wstate["cur"] == total_loads, (wstate, total_loads)
    S.emit(nc)
    _LAST[0] = S
    return nc


VEC_COLS = {}
BLK_NCOLS = {}


def _vec_layout():
    cols = {}
    c = 0

    def add(name, n):
        nonlocal c
        cols[name] = c
        c += n
    add("pool_scale", 4)
    add("conv_dw_w", 124)
    add("conv_dw_b", 4)
    add("conv_ln_g", 4)
    add("conv_ln_b", 4)
    for l in range(2):
        add("ln_mix_g%d" % l, 8)
        add("ln_mix_b%d" % l, 8)
        add("ffn_dw_w%d" % l, 66)
        add("ffn_dw_b%d" % l, 22)
        add("ple_b_gate%d" % l, 8)
        add("ln_ffn_g%d" % l, 8)
        add("ln_ffn_b%d" % l, 8)
    cols["_total"] = c
    return cols


VEC_COLS.update(_vec_layout())
for _b in range(NBLK):
    BLK_NCOLS[_b] = 4096
for _base in (B_FFN0 + 12, B_FFN0 + 17, B_FFN1 + 12, B_FFN1 + 17):
    for _i in range(4):
        BLK_NCOLS[_base + _i] = 2816
for _i in range(4):
    BLK_NCOLS[B_DIAG + _i] = 3968


def _v(a):
    a = np.asarray(a, np.float32)
    return a.reshape(-1, 128).T


def _fmtA(W):
    return W.reshape(8, 128, 512).transpose(1, 0, 2).reshape(128, 4096)


def _cols(W, starts):
    return np.concatenate([W[:, s:s + 128] for s in starts], axis=1)


def _fmtD(Wd, m):
    a = Wd[:, m * 128:(m + 1) * 128].reshape(NFF, 128, 128).transpose(1, 0, 2).reshape(128, DFF)
    out = np.zeros((128, 4096), np.float32)
    out[:, :DFF] = a
    return out


def _prep_shared(inp):
    f = lambda k: np.asarray(inp[k], np.float32)
    blocks = []
    w_in = f("mix_w_in")[0]
    blocks.append(_fmtA(w_in[:, 0:512]))
    for blk in range(2):
        starts = []
        for jj in range(2):
            j = blk * 2 + jj
            starts += [1024 + j * 128, 512 + j * 128]
        blocks.append(_fmtA(_cols(w_in, starts)))
    for _ in range(4):
        blocks.append(np.zeros((128, 4096), np.float32))
    w_out = f("mix_w_out")[0]
    blocks += [_fmtA(w_out[:, 0:512]), _fmtA(w_out[:, 512:1024])]

    def ffn_blocks(l):
        out = []
        wu = f("ffn_w_up")[l]
        for blk in range(11):
            starts = []
            for jj in range(2):
                j = blk * 2 + jj
                starts += [j * 128, DFF + j * 128]
            out.append(_fmtA(_cols(wu, starts)))
        wg = f("ple_w_gate")[l]
        wd = f("ffn_w_down")[l]
        for half in range(2):
            out.append(_fmtA(wg[:, half * 512:(half + 1) * 512]))
            for mi in range(4):
                out.append(_fmtD(wd, half * 4 + mi))
        return out
    blocks += ffn_blocks(0)
    wqkv = f("attn_w_qkv")[0]
    for s in (1024, 1536, 2048, 2560, 0, 512):
        blocks.append(_fmtA(wqkv[:, s:s + 512]))
    w_o = f("attn_w_o")[0]
    blocks += [_fmtA(w_o[:, 0:512]), _fmtA(w_o[:, 512:1024])]
    blocks += ffn_blocks(1)
    assert len(blocks) == NBLK
    wsrc = np.ascontiguousarray(np.concatenate(blocks, axis=0))

    pw = f("pool_w")[0].transpose(1, 0, 2).reshape(128, 512)
    pps = [f("ple_w_proj")[l].reshape(2, 128, 1024).transpose(1, 0, 2).reshape(128, 2048) for l in range(2)]
    wsm = np.ascontiguousarray(np.concatenate([pw] + pps, axis=1))

    vecs = np.zeros((128, VEC_COLS["_total"]), np.float32)

    def put(name, arr):
        c = VEC_COLS[name]
        vecs[:, c:c + arr.shape[1]] = arr
    put("pool_scale", _v(f("pool_scale")[0]))
    put("conv_dw_w", f("conv_dw_w")[0].reshape(31, 4, 128).transpose(2, 0, 1).reshape(128, 124))
    put("conv_dw_b", _v(f("conv_dw_b")[0]))
    put("conv_ln_g", _v(f("conv_ln_g")[0]))
    put("conv_ln_b", _v(f("conv_ln_b")[0]))
    for l in range(2):
        put("ln_mix_g%d" % l, _v(f("ln_mix_g")[l]))
        put("ln_mix_b%d" % l, _v(f("ln_mix_b")[l]))
        put("ffn_dw_w%d" % l, f("ffn_dw_w")[l].reshape(3, NFF, 128).transpose(2, 0, 1).reshape(128, 66))
        put("ffn_dw_b%d" % l, _v(f("ffn_dw_b")[l]))
        put("ple_b_gate%d" % l, _v(f("ple_b_gate")[l]))
        put("ln_ffn_g%d" % l, _v(f("ln_ffn_g")[l]))
        put("ln_ffn_b%d" % l, _v(f("ln_ffn_b")[l]))

    cst = np.zeros((128, 208), np.float32)
    cst[:, 0:128] = np.eye(128, dtype=np.float32)
    cst[:, 128:144] = (1.0 / np.arange(1, 17, dtype=np.float64)).astype(np.float32)[None, :]
    for _g, _w in enumerate((2, 4, 8, 16)):
        cst[:, 144 + _g * 16:160 + _g * 16] = (float(_w) / np.arange(1, 17, dtype=np.float64)).astype(np.float32)[None, :]

    rb = f("attn_rel_bias")[0]
    ki = np.arange(64)[:, None]
    qi = np.arange(64)[None, :]
    btab = np.full((128, 16, 10, 64), MASKV, np.float32)
    for r in range(10):
        if r <= 8:
            idx = np.clip(64 * r + qi - ki, -256, 256) + 256
            btab[0:64, :, r, :] = rb[:, idx].transpose(1, 0, 2)
        if r >= 1:
            idx = np.clip(64 * (r - 1) + qi - ki, -256, 256) + 256
            btab[64:128, :, r, :] = rb[:, idx].transpose(1, 0, 2)
    btab = np.ascontiguousarray(btab.reshape(128, 16 * 640))
    return dict(wsrc=wsrc, wsm=wsm, vecs=vecs, cst=cst, btab=btab)


_NC_CACHE = {}
_LAST = [None]


def kernel(_nt=None, **inp):
    NT = SEQ // T if _nt is None else _nt
    SC = NT * T
    shared = _prep_shared(inp)
    x = np.asarray(inp["x"], np.float32)
    p = np.asarray(inp["p"], np.float32)
    B = x.shape[0]
    in_maps = []
    for b in range(B):
        m = dict(shared)
        m["xT"] = np.ascontiguousarray(x[b, :SC].T)
        m["pT"] = np.ascontiguousarray(p[:, b, :SC].transpose(0, 2, 1).reshape(512, SC))
        in_maps.append(m)
    if NT not in _NC_CACHE:
        _NC_CACHE[NT] = build(NT)
    nc = _NC_CACHE[NT]
    res = run_bass_kernel_spmd(nc, in_maps, core_ids=list(range(B)))
    out = np.stack([np.asarray(r["outT"], np.float32).T for r in res.results], axis=0)
    return np.ascontiguousarray(out)
```
